# Optimizing a Trainium2 kernel written in Bass

```python
import jax, jax.numpy as jnp
from jax import lax
import numpy as np

D_MODEL = 1024
BATCH = 4
SEQ = 8192
DEPTH = 1

RET_HEADS = 4
RET_DK = 128
RET_DV = 256
RET_CHUNK = 128
SWA_HEADS = 8
SWA_KV_HEADS = 2
SWA_DH = 64
WINDOW = 128
D_FF = 4 * D_MODEL
EPS = 1e-6

RET_QK = RET_HEADS * RET_DK
RET_V = RET_HEADS * RET_DV
SWA_Q = SWA_HEADS * SWA_DH
SWA_KV = SWA_KV_HEADS * SWA_DH
IN_SPLITS = (RET_QK, RET_QK, RET_V, RET_V, SWA_Q, SWA_KV, SWA_KV, D_MODEL, D_MODEL)
D_IN = sum(IN_SPLITS)
SPLIT_IDX = tuple(int(i) for i in np.cumsum(IN_SPLITS)[:-1])

kernel_name = "hybrid_retention_swa_sinks_block"


def rmsnorm(x, g):
    xf = x.astype(jnp.float32)
    y = xf * lax.rsqrt(jnp.mean(xf * xf, axis=-1, keepdims=True) + EPS)
    return (y * g.astype(jnp.float32)).astype(x.dtype)


def retention_chunkwise(q, k, v):
    B, T, H, dk = q.shape
    dv = v.shape[-1]
    C = RET_CHUNK
    N = T // C
    f32 = jnp.float32
    log_gamma = jnp.log(1.0 - jnp.exp2(-5.0 - jnp.arange(H, dtype=f32)))
    q = q.astype(f32).reshape(B, N, C, H, dk)
    k = k.astype(f32).reshape(B, N, C, H, dk) * (dk ** -0.5)
    v = v.astype(f32).reshape(B, N, C, H, dv)
    pos = jnp.arange(C, dtype=f32)
    diff = pos[:, None] - pos[None, :]
    causal = diff >= 0
    decay = jnp.where(causal[None], jnp.exp(log_gamma[:, None, None] * jnp.where(causal, diff, 0.0)[None]), 0.0)
    qk = jnp.einsum('bnihd,bnjhd->bnhij', q, k) * decay[None, None]
    o_inner = jnp.einsum('bnhij,bnjhe->bnihe', qk, v)
    k_dec = k * jnp.exp(log_gamma[None, :] * (C - 1.0 - pos)[:, None])[None, None, :, :, None]
    kv = jnp.einsum('bnjhd,bnjhe->nbhde', k_dec, v)
    chunk_decay = jnp.exp(log_gamma * C)[None, :, None, None]

    def step(s, kv_n):
        return chunk_decay * s + kv_n, s

    _, s_prev = lax.scan(step, jnp.zeros((B, H, dk, dv), f32), kv)
    q_dec = q * jnp.exp(log_gamma[None, :] * (pos + 1.0)[:, None])[None, None, :, :, None]
    o_cross = jnp.einsum('bnihd,nbhde->bnihe', q_dec, s_prev)
    o = (o_inner + o_cross).reshape(B, T, H, dv)
    mu = jnp.mean(o, axis=-1, keepdims=True)
    var = jnp.mean(jnp.square(o - mu), axis=-1, keepdims=True)
    return (o - mu) * lax.rsqrt(var + EPS)


def swa_gqa_sinks(q, k, v, sinks):
    B, T, Hq, dh = q.shape
    G = k.shape[2]
    R = Hq // G
    W = WINDOW
    N = T // W
    f32 = jnp.float32
    qb = q.reshape(B, N, W, G, R, dh)
    kb = k.reshape(B, N, W, G, dh)
    vb = v.reshape(B, N, W, G, dh)
    pad = ((0, 0), (1, 0), (0, 0), (0, 0), (0, 0))
    kk = jnp.concatenate([jnp.pad(kb, pad)[:, :-1], kb], axis=2)
    vv = jnp.concatenate([jnp.pad(vb, pad)[:, :-1], vb], axis=2)
    s = jnp.einsum('bnigrd,bnjgd->bgrnij', qb, kk).astype(f32) * (dh ** -0.5)
    qpos = jnp.arange(W)[:, None] + W
    kpos = jnp.arange(2 * W)[None, :]
    dist = qpos - kpos
    valid = (dist >= 0) & (dist < W)
    blk_valid = valid[None] & ((jnp.arange(N)[:, None, None] > 0) | (kpos >= W)[None])
    slopes = jnp.exp2(-8.0 / Hq * jnp.arange(1, Hq + 1, dtype=f32)).reshape(G, R)
    s = s - slopes[None, :, :, None, None, None] * dist.astype(f32)[None, None, None, None]
    s = jnp.where(blk_valid[None, None, None], s, -jnp.inf)
    sink = sinks.astype(f32).reshape(G, R)[None, :, :, None, None, None]
    m = jnp.maximum(jnp.max(s, axis=-1, keepdims=True), sink)
    p = jnp.exp(s - m)
    p = p / (jnp.sum(p, axis=-1, keepdims=True) + jnp.exp(sink - m))
    out = jnp.einsum('bgrnij,bnjgd->bnigrd', p, vv.astype(f32))
    return out.reshape(B, T, Hq * dh)


def setup_inputs(seed: int = 0) -> dict:
    key = jax.random.key(seed)
    ks = jax.random.split(key, 13)
    f32 = jnp.float32

    def dense(k, fan_in, fan_out):
        return jax.random.normal(k, (DEPTH, fan_in, fan_out), f32) * (fan_in ** -0.5)

    def gain(k):
        return 1.0 + 0.02 * jax.random.normal(k, (DEPTH, D_MODEL), f32)

    return {
        "x": jax.random.normal(ks[0], (BATCH, SEQ, D_MODEL), f32),
        "pre_mix_norm": gain(ks[1]),
        "w_in": dense(ks[2], D_MODEL, D_IN),
        "w_ret_out": dense(ks[3], RET_V, D_MODEL),
        "w_swa_out": dense(ks[4], SWA_Q, D_MODEL),
        "w_out": dense(ks[5], D_MODEL, D_MODEL),
        "sinks": 0.5 * jax.random.normal(ks[6], (DEPTH, SWA_HEADS), f32),
        "post_mix_norm": gain(ks[7]),
        "pre_mlp_norm": gain(ks[8]),
        "w_up": dense(ks[9], D_MODEL, D_FF),
        "w_down": dense(ks[10], D_FF, D_MODEL),
        "post_mlp_norm": gain(ks[11]),
    }


def reference(x, pre_mix_norm, w_in, w_ret_out, w_swa_out, w_out, sinks, post_mix_norm, pre_mlp_norm, w_up, w_down, post_mlp_norm):
    B, T, _ = x.shape
    dt = x.dtype
    for l in range(DEPTH):
        h = rmsnorm(x, pre_mix_norm[l])
        proj = h @ w_in[l]
        q_r, k_r, v_r, g_r, q_s, k_s, v_s, gate_r, gate_s = jnp.split(proj, SPLIT_IDX, axis=-1)
        ret = retention_chunkwise(q_r.reshape(B, T, RET_HEADS, RET_DK),
                                  k_r.reshape(B, T, RET_HEADS, RET_DK),
                                  v_r.reshape(B, T, RET_HEADS, RET_DV)).reshape(B, T, RET_V)
        ret = (jax.nn.silu(g_r.astype(jnp.float32)) * ret).astype(dt)
        y_r = ret @ w_ret_out[l]
        swa = swa_gqa_sinks(q_s.reshape(B, T, SWA_HEADS, SWA_DH),
                            k_s.reshape(B, T, SWA_KV_HEADS, SWA_DH),
                            v_s.reshape(B, T, SWA_KV_HEADS, SWA_DH),
                            sinks[l]).astype(dt)
        y_s = swa @ w_swa_out[l]
        merged = jax.nn.sigmoid(gate_r) * y_r + jax.nn.sigmoid(gate_s) * y_s
        x = x + rmsnorm(merged @ w_out[l], post_mix_norm[l])
        h = rmsnorm(x, pre_mlp_norm[l])
        u = jnp.square(jax.nn.relu(h @ w_up[l]))
        x = x + rmsnorm(u @ w_down[l], post_mlp_norm[l])
    return x
```

```python
import numpy as np
import ml_dtypes
import concourse.bass as bass
import concourse.mybir as mybir
from concourse.bass_utils import run_bass_kernel_spmd

F32 = mybir.dt.float32
BF16 = mybir.dt.bfloat16
ALU = mybir.AluOpType
AF = mybir.ActivationFunctionType

D = 1024
EPS = 1e-6
ENGS = ("pe", "act", "dve", "pool", "sp")


class Buf:
    __slots__ = ("name", "lw", "rd", "rd_dma", "excl")

    def __init__(self, name, excl=False):
        self.name = name
        self.excl = excl
        self.lw = None
        self.rd = {}
        self.rd_dma = []


class Op:
    __slots__ = ("eng", "fn", "dma", "idx", "deps_c", "deps_d", "needs_inc", "seq", "dsem", "dval", "dprev")

    def __init__(self, eng, fn, dma):
        self.eng = eng
        self.fn = fn
        self.dma = dma
        self.needs_inc = False
        self.seq = None
        self.dsem = None
        self.dval = None
        self.dprev = None


class MK:
    DMA_SLOTS = {"sp": 24, "pool": 16}

    def __init__(self, nc):
        self.nc = nc
        self.ops = {e: [] for e in ENGS}
        self.ndma = {e: 0 for e in ENGS}
        self.last_dma = {}
        self.bar = None
        self.bar_pending = set()

    def barrier(self):
        last_c = {e: self.ops[e][-1] for e in ("pe", "act", "dve", "pool")
                  if any(not o.dma for o in self.ops[e])}
        for e in list(last_c):
            o = [o for o in self.ops[e] if not o.dma][-1]
            last_c[e] = o
        self.bar = (last_c, list(self.last_dma.values()))
        self.bar_pending = set(ENGS)

    def add(self, eng, fn, reads=(), writes=(), dma=False):
        op = Op(eng, fn, dma)
        op.idx = len(self.ops[eng])
        deps_c = {}
        deps_d = []

        def dep(o):
            if o is None:
                return
            if o.dma:
                if o not in deps_d:
                    deps_d.append(o)
            else:
                cur = deps_c.get(o.eng)
                if cur is None or o.idx > cur.idx:
                    deps_c[o.eng] = o

        for b in reads:
            dep(b.lw)
            if b.excl:
                for e2, o in b.rd.items():
                    if e2 != eng:
                        dep(o)
        for b in writes:
            dep(b.lw)
            for o in b.rd.values():
                dep(o)
            for o in b.rd_dma:
                dep(o)
        if eng in self.bar_pending:
            self.bar_pending.discard(eng)
            for o in self.bar[0].values():
                dep(o)
            for o in self.bar[1]:
                dep(o)
        if eng == "pe":
            deps_c.pop("pe", None)
        if dma:
            n = self.DMA_SLOTS[eng]
            self.last_dma[(eng, self.ndma[eng] % n)] = op
            self.ndma[eng] += 1
        for o in deps_c.values():
            o.needs_inc = True
        op.deps_c = deps_c
        op.deps_d = deps_d
        for b in reads:
            if dma:
                b.rd_dma.append(op)
            else:
                b.rd[eng] = op
        for b in writes:
            b.lw = op
            b.rd = {}
            b.rd_dma = []
        self.ops[eng].append(op)
        return op

    def emit(self):
        n_dma_sems = tuple(self.DMA_SLOTS.items())
        nc = self.nc
        prog = {e: nc.alloc_semaphore("prog_" + e) for e in ("pe", "act", "dve", "pool")}
        pools = {q: [nc.alloc_semaphore("dma_%s_%d" % (q, i)) for i in range(n)] for q, n in n_dma_sems}
        for e in ENGS:
            cnt = 0
            ndma = 0
            last = {}
            for op in self.ops[e]:
                if op.dma:
                    pool = pools[e]
                    i = ndma % len(pool)
                    ndma += 1
                    op.dsem = pool[i]
                    prev = last.get(i, 0)
                    op.dprev = prev
                    op.dval = prev + 16
                    last[i] = op.dval
                elif op.needs_inc:
                    cnt += 1
                    op.seq = cnt
        handles = {"pe": nc.tensor, "act": nc.scalar, "dve": nc.vector, "pool": nc.gpsimd, "sp": nc.sync}
        final_dma = []
        for e in ENGS:
            for op in self.ops[e]:
                if op.dma:
                    final_dma.append(op)

        def run_engine(e, h):
            waited = {}

            def wait(sem, val):
                key = id(sem)
                if waited.get(key, 0) >= val:
                    return
                h.wait_ge(sem, val)
                waited[key] = val

            for op in self.ops[e]:
                for d in op.deps_c.values():
                    wait(prog[d.eng], d.seq)
                for d in op.deps_d:
                    wait(d.dsem, d.dval)
                if op.dma:
                    if op.dprev:
                        wait(op.dsem, op.dprev)
                    ins = op.fn(h)
                    ins.then_inc(op.dsem, 16)
                else:
                    ins = op.fn(h)
                    if op.needs_inc:
                        ins.then_inc(prog[e], 1)
            if e == "sp":
                lastval = {}
                for op in final_dma:
                    k = id(op.dsem)
                    if k not in lastval or lastval[k][1] < op.dval:
                        lastval[k] = (op.dsem, op.dval)
                for sem, val in lastval.values():
                    wait(sem, val)

        with nc.Block() as block:
            @block.tensor
            def _(h):
                run_engine("pe", h)

            @block.scalar
            def _(h):
                run_engine("act", h)

            @block.vector
            def _(h):
                run_engine("dve", h)

            @block.gpsimd
            def _(h):
                run_engine("pool", h)

            @block.sync
            def _(h):
                run_engine("sp", h)


class T:
    __slots__ = ("t", "b")

    def __init__(self, t, name, excl=False):
        self.t = t
        self.b = Buf(name, excl)


def _dsize(dt):
    return 4 if dt == F32 else 2


class SBAlloc:
    def __init__(self, nc):
        self.nc = nc
        self.off = 16384 + 256
        self.n = 0
        self.peak = 0

    def alloc(self, shape, dtype, name="t"):
        nb = _dsize(dtype)
        for s in shape[1:]:
            nb *= s
        nb = (nb + 31) // 32 * 32
        t = self.nc.alloc_sbuf_tensor_at("%s_%d" % (name, self.n), list(shape), dtype, offset=self.off)
        self.n += 1
        self.off += nb
        self.peak = max(self.peak, self.off)
        assert self.off <= 229376, ("SBUF overflow", self.off)
        return T(t, name)

    def mark(self):
        return self.off

    def reset(self, m):
        self.off = m


def _consts(first_half):
    H = 4
    C = 128
    lg = np.log(1.0 - np.exp2(-5.0 - np.arange(H, dtype=np.float64)))
    i = np.arange(C, dtype=np.float64)
    dk = 128.0 ** -0.5
    diff = i[None, :] - i[:, None]
    DT = np.zeros((C, H, C))
    for h in range(H):
        DT[:, h, :] = np.where(diff >= 0, np.exp(lg[h] * np.where(diff >= 0, diff, 0.0)), 0.0) * dk
    GQ = np.zeros((C, H, C))
    for h in range(H):
        GQ[:, h, :] = np.exp(lg[h] * (i + 1.0))[None, :]
    KD = np.zeros((C, H, C))
    for h in range(H):
        KD[:, h, :] = (np.exp(lg[h] * (C - 1.0 - i)) * dk)[:, None]
    cd = [float(np.exp(lg[h] * C)) for h in range(H)]
    Hq = 8
    slopes = np.exp2(-8.0 / Hq * np.arange(1, Hq + 1, dtype=np.float64))
    E = np.zeros((2, 2, C, 4, C))
    for g in range(2):
        for r in range(4):
            sl = slopes[g * 4 + r]
            dist = i[None, :] - i[:, None]
            E[g, 1, :, r, :] = np.where(dist >= 0, np.exp(-sl * np.where(dist >= 0, dist, 0.0)), 0.0)
            dist = i[None, :] + 128.0 - i[:, None]
            E[g, 0, :, r, :] = np.where(dist < 128, np.exp(-sl * np.where(dist < 128, dist, 0.0)), 0.0)
    E0 = E[:, 0].copy()
    if first_half:
        E0[:] = 0.0
    ident = np.eye(128, dtype=np.float32).astype(ml_dtypes.bfloat16)
    return dict(
        c_dt=DT.reshape(C, 512).astype(np.float32),
        c_gq=GQ.reshape(C, 512).astype(np.float32),
        c_kd=KD.reshape(C, 512).astype(np.float32),
        c_e=np.ascontiguousarray(E.transpose(2, 0, 1, 3, 4)).reshape(C, 4 * 512).astype(np.float32),
        c_e0=np.ascontiguousarray(E0.transpose(1, 0, 2, 3)).reshape(C, 2 * 512).astype(np.float32),
        c_ident=ident,
    ), cd


def build(NT=32, NP=32, phases="PABC", dbg=False):
    nc = bass.Bass("TRN2", target_bir_lowering=False)
    mk = MK(nc)
    sb = SBAlloc(nc)
    _, cd = _consts(True)
    Tn = NT * 128

    def din(name, shape, dt=F32):
        return nc.dram_tensor(name, list(shape), dt, kind="ExternalInput").ap()

    x_d = din("x", [Tn, D])
    xp_d = din("xprev", [NP * 128, D])
    w_in_d = din("w_in", [D, 5888])
    w_ro_d = din("w_ret_out", [1024, D])
    w_so_d = din("w_swa_out", [512, D])
    w_o_d = din("w_out", [D, D])
    w_up_d = din("w_up", [D, 4096])
    w_dn_d = din("w_down", [4096, D])
    g_pre_d = din("pre_mix_norm", [1, D])
    g_pm_d = din("post_mix_norm", [1, D])
    g_pl_d = din("pre_mlp_norm", [1, D])
    g_pml_d = din("post_mlp_norm", [1, D])
    sinks_d = din("sinks", [1, 8])
    c_dt_d = din("c_dt", [128, 512])
    c_gq_d = din("c_gq", [128, 512])
    c_kd_d = din("c_kd", [128, 512])
    c_e_d = din("c_e", [128, 2048])
    c_e0_d = din("c_e0", [128, 1024])
    c_id_d = din("c_ident", [128, 128], BF16)
    out_d = nc.dram_tensor("out", [Tn, D], F32, kind="ExternalOutput").ap()
    skind = "ExternalOutput" if dbg else "Internal"
    xT_d = nc.dram_tensor("xT_s", [NT, 128, 8 * 128], BF16, kind=skind).ap()
    retT_d = nc.dram_tensor("retT_s", [NT, 128, 8 * 128], BF16, kind=skind).ap()
    swaT_d = nc.dram_tensor("swaT_s", [NT, 128, 4 * 128], BF16, kind=skind).ap()
    dram_b = {}

    def dbuf(name):
        if name not in dram_b:
            dram_b[name] = Buf(name)
        return dram_b[name]

    dumped = set()

    def dump(name, ap, shape, dt, reads):
        if not dbg or name in dumped:
            return
        dumped.add(name)
        d_ = nc.dram_tensor("dbg_" + name, list(shape), dt, kind="ExternalOutput").ap()
        mk.add("sp", lambda e: e.dma_start(out=d_, in_=ap), reads=reads, writes=[dbuf("dbg_" + name)], dma=True)

    pfB_global = {}
    pfC_global = {}
    banks = [T(nc.alloc_psum_tensor("bank%d" % i, [128, 512], F32), "bank%d" % i, True) for i in range(8)]
    bank_ctr = [0]
    bank_pool = [list(range(8))]

    def next_bank():
        p = bank_pool[0]
        b = banks[p[bank_ctr[0] % len(p)]]
        bank_ctr[0] += 1
        return b

    def bf(bank):
        return bank.t[:].bitcast(BF16)

    ident = sb.alloc([128, 128], BF16, "ident")
    mk.add("sp", lambda e: e.dma_start(out=ident.t[:], in_=c_id_d[:, :]), writes=[ident.b], dma=True)
    base_mark = sb.mark()

    def load_bcast(dst, src_d):
        mk.add("sp", lambda e: e.dma_start(out=dst.t[:], in_=src_d[0:1, :].partition_broadcast(128)),
               writes=[dst.b], dma=True)

    def load_w(dst_ap, src_ap, name):
        b_ = Buf(name)
        mk.add("pool", lambda e: e.dma_start(out=dst_ap, in_=src_ap), writes=[b_], dma=True)
        return b_

    def rms_front(xin, xs, gam, tag):
        ss = small(tag + "ss")
        rs = small(tag + "rs")
        mk.add("act", lambda e: e.activation(out=xs.t[:], in_=xin.t[:], func=AF.Square, accum_out=ss.t[:, 0:1]),
               reads=[xin.b], writes=[xs.b, ss.b])
        mk.add("pool", lambda e: e.tensor_scalar(out=rs.t[:, 0:1], in0=ss.t[:, 0:1], scalar1=1.0 / D, scalar2=EPS,
                                                 op0=ALU.mult, op1=ALU.add),
               reads=[ss.b], writes=[rs.b])
        mk.add("pool", lambda e: e.tensor_tensor(out=rs.t[:, 1:2], in0=rs.t[:, 0:1], in1=mhalf.t[:, 0:1], op=ALU.pow),
               reads=[rs.b, mhalf.b], writes=[rs.b])
        mk.add("dve", lambda e: e.scalar_tensor_tensor(out=xs.t[:], in0=xin.t[:], scalar=rs.t[:, 1:2],
                                                       in1=gam.t[:], op0=ALU.mult, op1=ALU.mult),
               reads=[xin.b, rs.b, gam.b], writes=[xs.b])

    def transpose8(src, dst_ap, dst, eng="act"):
        bk = next_bank()
        bv = bf(bk)

        def f(e):
            ins = None
            for k in range(8):
                ins = e.transpose(out=bv[:, k * 128:(k + 1) * 128], in_=src.t[:, k * 128:(k + 1) * 128],
                                  identity=ident.t[:])
            return ins

        mk.add("pe", f, reads=[src.b, ident.b], writes=[bk.b])
        src_ap = bv[:, 0:1024].rearrange("p (k t) -> p k t", k=8)
        if eng == "act":
            mk.add("act", lambda e: e.copy(out=dst_ap, in_=src_ap), reads=[bk.b], writes=[dst.b])
        else:
            mk.add("dve", lambda e: e.tensor_copy(out=dst_ap, in_=src_ap), reads=[bk.b], writes=[dst.b])

    def post_norm_residual(bk0, bk1, xres, gam, xo, tag):
        ss = small(tag + "ss")
        rs = small(tag + "rs")
        junk = junk_t
        mk.add("act", lambda e: e.activation(out=junk.t[:, 0:512], in_=bk0.t[:], func=AF.Square,
                                             accum_out=ss.t[:, 0:1]),
               reads=[bk0.b], writes=[junk.b, ss.b])
        mk.add("act", lambda e: e.activation(out=junk.t[:, 512:1024], in_=bk1.t[:], func=AF.Square,
                                             accum_out=ss.t[:, 1:2]),
               reads=[bk1.b], writes=[junk.b, ss.b])
        mk.add("dve", lambda e: e.tensor_tensor(out=ss.t[:, 2:3], in0=ss.t[:, 0:1], in1=ss.t[:, 1:2], op=ALU.add),
               reads=[ss.b], writes=[ss.b])
        mk.add("pool", lambda e: e.tensor_scalar(out=rs.t[:, 0:1], in0=ss.t[:, 2:3], scalar1=1.0 / D, scalar2=EPS,
                                                 op0=ALU.mult, op1=ALU.add),
               reads=[ss.b], writes=[rs.b])
        mk.add("pool", lambda e: e.tensor_tensor(out=rs.t[:, 1:2], in0=rs.t[:, 0:1], in1=mhalf.t[:, 0:1], op=ALU.pow),
               reads=[rs.b, mhalf.b], writes=[rs.b])
        mk.add("dve", lambda e: e.scalar_tensor_tensor(out=xo.t[:, 0:512], in0=bk0.t[:], scalar=rs.t[:, 1:2],
                                                       in1=gam.t[:, 0:512], op0=ALU.mult, op1=ALU.mult),
               reads=[bk0.b, rs.b, gam.b], writes=[xo.b])
        mk.add("dve", lambda e: e.scalar_tensor_tensor(out=xo.t[:, 512:1024], in0=bk1.t[:], scalar=rs.t[:, 1:2],
                                                       in1=gam.t[:, 512:1024], op0=ALU.mult, op1=ALU.mult),
               reads=[bk1.b, rs.b, gam.b], writes=[xo.b])
        dump(tag + "_zn", xo.t[:], [128, 1024], F32, [xo.b])
        dump(tag + "_gam", gam.t[:], [128, 1024], F32, [gam.b])
        dump(tag + "_rs", rs.t[:], [128, 4], F32, [rs.b])
        dump(tag + "_ss", ss.t[:], [128, 4], F32, [ss.b])
        if False and dbg and (tag + "_zr") not in dumped:
            zr = sb.alloc([128, 1024], F32, "zr")
            mk.add("act", lambda e: e.copy(out=zr.t[:, 0:512], in_=bk0.t[:]), reads=[bk0.b], writes=[zr.b])
            mk.add("act", lambda e: e.copy(out=zr.t[:, 512:1024], in_=bk1.t[:]), reads=[bk1.b], writes=[zr.b])
            dump(tag + "_zr", zr.t[:], [128, 1024], F32, [zr.b])
        mk.add("dve", lambda e: e.tensor_tensor(out=xo.t[:], in0=xo.t[:], in1=xres.t[:], op=ALU.add),
               reads=[xo.b, xres.b], writes=[xo.b])

    small_ring = {}

    def small(tag):
        if tag not in small_ring:
            small_ring[tag] = ([sb_small.alloc([128, 4], F32, tag) for _ in range(4)], [0])
        lst, i = small_ring[tag]
        t = lst[i[0] % 4]
        i[0] += 1
        return t

    sb_small = sb
    epst = sb.alloc([128, 1], F32, "eps")
    mk.add("pool", lambda e: e.memset(epst.t[:], EPS), writes=[epst.b])
    junk_t = sb.alloc([128, 1024], BF16, "junk")
    mhalf = sb.alloc([128, 4], F32, "mhalf")
    mk.add("pool", lambda e: e.memset(mhalf.t[:], -0.5), writes=[mhalf.b])
    if "A" in phases or "P" in phases:
        m0 = sb.mark()
        winA_off = sb.off
        winA = sb.alloc([128, 8, 3840], BF16, "winA")
        sb_fence = sb.alloc([128, 8], F32, "fence")
        gpre = sb.alloc([128, D], F32, "gpre")
        c_dt = sb.alloc([128, 512], F32, "c_dt")
        c_gq = sb.alloc([128, 512], F32, "c_gq")
        c_kd = sb.alloc([128, 512], F32, "c_kd")
        c_e = sb.alloc([128, 2048], F32, "c_e")
        c_e0 = sb.alloc([128, 1024], F32, "c_e0")
        ones64 = sb.alloc([128, 64], BF16, "ones64")
        sinkc = sb.alloc([128, 512], F32, "sinkc")
        sk8 = sb.alloc([128, 8], F32, "sk8")
        S = sb.alloc([128, 1024], F32, "S")
        Sbf = [sb.alloc([128, 1024], BF16, "Sbf%d" % i) for i in range(2)]
        xin_r = [sb.alloc([128, D], F32, "xin") for _ in range(4)]
        xs_r = [sb.alloc([128, D], BF16, "xs") for _ in range(2)]
        xT_r = [sb.alloc([128, 8, 128], BF16, "xT") for _ in range(3)]
        qrT_r = [sb.alloc([128, 4, 128], BF16, "qrT") for _ in range(2)]
        qdT_r = [sb.alloc([128, 4, 128], BF16, "qdT") for _ in range(2)]
        krT_r = [sb.alloc([128, 4, 128], BF16, "krT") for _ in range(2)]
        qsT_r = [sb.alloc([128, 4, 128], BF16, "qsT") for _ in range(2)]
        ksT_r = [[sb.alloc([128, 128], BF16, "ksT") for g in range(2)] for _ in range(3)]
        vs_r = [[sb.alloc([128, 128], BF16, "vs") for g in range(2)] for _ in range(3)]
        onesm = [sb.alloc([128, 128], BF16, "onesm") for g in range(2)]
        kdec_r = [sb.alloc([128, 512], BF16, "kdec") for _ in range(2)]
        v_r = [sb.alloc([128, 1024], BF16, "v") for _ in range(2)]
        sg_r = [sb.alloc([128, 1024], F32, "sg") for _ in range(3)]
        PT_r = [sb.alloc([128, 4, 128], BF16, "PT") for _ in range(2)]
        on_r = [sb.alloc([128, 1024], F32, "on") for _ in range(2)]
        ret_r = [sb.alloc([128, 1024], BF16, "ret") for _ in range(2)]
        retT_r = [sb.alloc([128, 8, 128], BF16, "retT") for _ in range(2)]
        es_r = [sb.alloc([128, 512], F32, "es") for _ in range(2)]
        pT_r = [sb.alloc([128, 512], BF16, "pT") for _ in range(4)]
        dn_r = [sb.alloc([128, 512], F32, "dn") for _ in range(2)]
        dn2_r = [sb.alloc([128, 512], F32, "dn2") for _ in range(2)]
        botS_r = [sb.alloc([128, 512], F32, "botS") for _ in range(2)]
        swaT_r = [sb.alloc([128, 512], BF16, "swaT") for _ in range(2)]
        gn_st = [sb.alloc([128, 4, 6], F32, "gnst") for _ in range(2)]
        gn_mv = [sb.alloc([128, 4, 2], F32, "gnmv") for _ in range(2)]
        gn_rs = [sb.alloc([128, 4], F32, "gnrs") for _ in range(2)]
        gn_nb = [sb.alloc([128, 4], F32, "gnnb") for _ in range(2)]

        for t_, d_ in ((c_dt, c_dt_d), (c_gq, c_gq_d), (c_kd, c_kd_d), (c_e, c_e_d), (c_e0, c_e0_d)):
            mk.add("sp", (lambda t_, d_: lambda e: e.dma_start(out=t_.t[:], in_=d_[:, :]))(t_, d_),
                   writes=[t_.b], dma=True)
        load_bcast(gpre, g_pre_d)
        mk.add("sp", lambda e: e.dma_start(out=sk8.t[:], in_=sinks_d[0:1, :].partition_broadcast(128)),
               writes=[sk8.b], dma=True)
        col_order = [("kr", 512, 1024), ("vr", 1024, 2048), ("ksvs", 3584, 3840), ("qr", 0, 512),
                     ("gr", 2048, 3072), ("qs", 3072, 3584)]
        wA = {}
        wq = []
        for (nm, c0, c1) in col_order:
            wA[nm] = []
            for k in range(8):
                b_ = Buf("winA_%s_%d" % (nm, k))
                wA[nm].append(b_)
                if nm == "qs":
                    for t4 in range(4):
                        if t4 > 0:
                            b_ = Buf("winA_qs_%d_%d" % (k, t4))
                            wA[nm].append(b_)
                        wq.append((nm, (lambda k, t4: lambda e: e.dma_start(
                            out=winA.t[:, k, 3072 + t4 * 128:3072 + (t4 + 1) * 128].rearrange("p (g d) -> p g d", g=2, d=64),
                            in_=w_in_d[k * 128:(k + 1) * 128, 3072:3584].rearrange(
                                "p (g t d) -> p t g d", g=2, t=4, d=64)[:, t4]))(k, t4), b_))
                    continue
                wq.append((nm, (lambda k, c0, c1: lambda e: e.dma_start(
                    out=winA.t[:, k, c0:c1], in_=w_in_d[k * 128:(k + 1) * 128, c0:c1]))(k, c0, c1), b_))

        def emit_w(n_):
            for _ in range(n_):
                if wq:
                    nm_, fn_, b__ = wq.pop(0)
                    mk.add("pool", fn_, writes=[b__], dma=True)
        mk.add("pool", lambda e: e.memset(ones64.t[:], 1.0), writes=[ones64.b])
        for g in range(2):
            mk.add("pool", (lambda g: lambda e: e.memset(onesm[g].t[:], 0.0))(g), writes=[onesm[g].b])
            mk.add("pool", (lambda g: lambda e: e.memset(onesm[g].t[:, g * 64:(g + 1) * 64], 1.0))(g), writes=[onesm[g].b])
            for i3 in range(3):
                mk.add("pool", (lambda t_: lambda e: e.memset(t_.t[:], 0.0))(ksT_r[i3][g]), writes=[ksT_r[i3][g].b])
                mk.add("pool", (lambda t_: lambda e: e.memset(t_.t[:], 0.0))(vs_r[i3][g]), writes=[vs_r[i3][g].b])
        mk.add("pool", lambda e: e.memset(S.t[:], 0.0), writes=[S.b])
        mk.add("pool", lambda e: e.memset(Sbf[0].t[:], 0.0), writes=[Sbf[0].b])
        mk.add("act", lambda e: e.activation(out=sk8.t[:], in_=sk8.t[:], func=AF.Exp), reads=[sk8.b], writes=[sk8.b])
        for g in range(2):
            for r in range(4):
                mk.add("dve", (lambda g, r: lambda e: e.tensor_copy(
                    out=sinkc.t[g * 64:(g + 1) * 64, r * 128:(r + 1) * 128],
                    in_=sk8.t[g * 64:(g + 1) * 64, g * 4 + r:g * 4 + r + 1].to_broadcast([64, 128])))(g, r),
                    reads=[sk8.b], writes=[sinkc.b])

        QR, KR, VR, GR, QS, KS, VS = 0, 512, 1024, 2048, 3072, 3584, 3712

        R = {}
        winA_guard = Buf("winA_guard")
        pfB = pfB_global

        def prefetch_B():
            if "B" not in phases:
                return
            fence = sb_fence
            mk.add("pool", lambda e: e.memset(fence.t[:], 0.0), writes=[fence.b, winA_guard])
            base = winA_off
            wro_ = T(nc.alloc_sbuf_tensor_at("wro_pf", [128, 8, 1024], BF16, offset=base), "wro")
            wso_ = T(nc.alloc_sbuf_tensor_at("wso_pf", [128, 4, 1024], BF16, offset=base + 16384), "wso")
            wg_ = T(nc.alloc_sbuf_tensor_at("wg_pf", [128, 8, 2048], BF16, offset=base + 24576), "wg")
            pfB["wro"], pfB["wso"], pfB["wg"] = wro_, wso_, wg_
            pfB["b_wro"] = [load_w(wro_.t[:, k, :], w_ro_d[k * 128:(k + 1) * 128, :], "wro%d" % k) for k in range(8)]
            pfB["b_wso"] = [load_w(wso_.t[g * 64:(g + 1) * 64, :, :],
                                   w_so_d[g * 256:(g + 1) * 256, :].rearrange("(r d) c -> d r c", r=4, d=64), "wso%d" % g)
                            for g in range(2)]
            bl = []
            for k in range(8):
                for hf in range(2):
                    bl.append(load_w(wg_.t[:, k, hf * 1024:(hf + 1) * 1024],
                                     w_in_d[k * 128:(k + 1) * 128, 3840 + hf * 1024:3840 + (hf + 1) * 1024], "wg"))
            pfB["b_wg"] = bl

        pending_stores = []

        def flush_stores():
            for fn, rd, wr in pending_stores:
                mk.add("sp", fn, reads=rd, writes=wr, dma=True)
            del pending_stores[:]

        def load(ci, src_ap):
            xin = xin_r[ci % 4]
            mk.add("sp", lambda e: e.dma_start(out=xin.t[:], in_=src_ap), writes=[xin.b], dma=True)

        def norm(ci):
            xin = xin_r[ci % 4]
            xs = xs_r[ci % 2]
            rms_front(xin, xs, gpre, "f")
            R[ci] = dict(xs=xs)

        def tr(ci):
            xT = xT_r[ci % 3]
            transpose8(R[ci]["xs"], xT.t[:], xT, "act")
            R[ci]["xT"] = xT
            if ci >= 0:
                pending_stores.append((lambda e: e.dma_start(out=xT_d[ci], in_=xT.t[:].rearrange("p k t -> p (k t)")),
                                       [xT.b], [dbuf("xT%d" % ci)]))

        def proj(ci, full, want_swa_kv):
            r = R[ci]
            xT = r["xT"]

            def proj_tok(c0, ncols, bk, boff=0, wb=()):
                def f(e):
                    ins = None
                    for k in range(8):
                        ins = e.matmul(bk.t[:, boff:boff + ncols], lhsT=xT.t[:, k, :], rhs=winA.t[:, k, c0:c0 + ncols],
                                       start=(k == 0), stop=(k == 7))
                    return ins
                mk.add("pe", f, reads=[xT.b, winA_guard] + list(wb), writes=[bk.b])

            def proj_feat(c0, bk, boff, wb=()):
                def f(e):
                    ins = None
                    for k in range(8):
                        ins = e.matmul(bk.t[:, boff:boff + 128], lhsT=winA.t[:, k, c0:c0 + 128], rhs=xT.t[:, k, :],
                                       start=(k == 0), stop=(k == 7))
                    return ins
                mk.add("pe", f, reads=[xT.b, winA_guard] + list(wb), writes=[bk.b])

            if full:
                qrT = qrT_r[ci % 2]
                qdT = qdT_r[ci % 2]
                krT = krT_r[ci % 2]
                qsT = qsT_r[ci % 2]
                bq = next_bank()
                for h in range(4):
                    proj_feat(QR + h * 128, bq, h * 128, wA["qr"])
                mk.add("act", lambda e: e.copy(out=qrT.t[:].rearrange("p h t -> p (h t)"), in_=bq.t[:]),
                       reads=[bq.b], writes=[qrT.b])
                mk.add("dve", lambda e: e.tensor_tensor(out=qdT.t[:].rearrange("p h t -> p (h t)"), in0=bq.t[:],
                                                        in1=c_gq.t[:], op=ALU.mult),
                       reads=[bq.b, c_gq.b], writes=[qdT.b])
                bkk = next_bank()
                for h in range(4):
                    proj_feat(KR + h * 128, bkk, h * 128, wA["kr"])
                mk.add("dve", lambda e: e.tensor_copy(out=krT.t[:].rearrange("p h t -> p (h t)"), in_=bkk.t[:]),
                       reads=[bkk.b], writes=[krT.b])
                bqs = next_bank()
                for t4 in range(4):
                    proj_feat(QS + t4 * 128, bqs, t4 * 128, wA["qs"])
                mk.add("act", lambda e: e.copy(out=qsT.t[:].rearrange("p h t -> p (h t)"), in_=bqs.t[:]),
                       reads=[bqs.b], writes=[qsT.b])
                r.update(qrT=qrT, qdT=qdT, krT=krT, qsT=qsT)
            if want_swa_kv:
                ksT = ksT_r[ci % 3]
                vs = vs_r[ci % 3]
                bks = next_bank()
                proj_feat(KS, bks, 0, wA["ksvs"])
                proj_tok(VS, 128, bks, 128, wA["ksvs"])
                for g in range(2):
                    mk.add("dve", (lambda g: lambda e: e.tensor_copy(out=ksT[g].t[g * 64:(g + 1) * 64, :],
                                                                     in_=bks.t[g * 64:(g + 1) * 64, 0:128]))(g),
                           reads=[bks.b], writes=[ksT[g].b])
                    mk.add("dve", (lambda g: lambda e: e.tensor_copy(out=vs[g].t[:, g * 64:(g + 1) * 64],
                                                                     in_=bks.t[:, 128 + g * 64:128 + (g + 1) * 64]))(g),
                           reads=[bks.b], writes=[vs[g].b])
                r.update(ksT=ksT, vs=vs)
            kdec = kdec_r[ci % 2]
            v = v_r[ci % 2]
            bk = next_bank()
            proj_tok(KR, 512, bk, 0, wA["kr"])
            mk.add("dve", lambda e: e.tensor_tensor(out=kdec.t[:], in0=bk.t[:], in1=c_kd.t[:], op=ALU.mult),
                   reads=[bk.b, c_kd.b], writes=[kdec.b])
            for hf in range(2):
                bkv = next_bank()
                proj_tok(VR + hf * 512, 512, bkv, 0, wA["vr"])
                if hf == 0:
                    mk.add("act", (lambda bkv, hf: lambda e: e.copy(out=v.t[:, hf * 512:(hf + 1) * 512], in_=bkv.t[:]))(bkv, hf),
                           reads=[bkv.b], writes=[v.b])
                else:
                    mk.add("dve", (lambda bkv, hf: lambda e: e.tensor_copy(out=v.t[:, hf * 512:(hf + 1) * 512], in_=bkv.t[:]))(bkv, hf),
                           reads=[bkv.b], writes=[v.b])
            r.update(kdec=kdec, v=v)
            if full:
                sg = sg_r[ci % 3]
                for hf in range(2):
                    bg = next_bank()
                    proj_tok(GR + hf * 512, 512, bg, 0, wA["gr"])
                    mk.add("act", (lambda bg, hf: lambda e: e.activation(out=sg.t[:, hf * 512:(hf + 1) * 512], in_=bg.t[:],
                                                                         func=AF.Silu))(bg, hf),
                           reads=[bg.b], writes=[sg.b])
                r.update(sg=sg)

        def kv(ci, sb_next):
            r = R[ci]
            kdec, v = r["kdec"], r["v"]
            bks2 = [banks[4], banks[5]]
            for h in range(4):
                bk = bks2[h // 2]
                mk.add("pe", (lambda h, bk: lambda e: e.matmul(
                    bk.t[:, (h % 2) * 256:(h % 2 + 1) * 256], lhsT=kdec.t[:, h * 128:(h + 1) * 128],
                    rhs=v.t[:, h * 256:(h + 1) * 256], start=True, stop=True))(h, bk),
                    reads=[kdec.b, v.b], writes=[bk.b])
            for h in range(4):
                bk = bks2[h // 2]
                mk.add("dve", (lambda h, bk: lambda e: e.scalar_tensor_tensor(
                    out=S.t[:, h * 256:(h + 1) * 256], in0=S.t[:, h * 256:(h + 1) * 256], scalar=cd[h],
                    in1=bk.t[:, (h % 2) * 256:(h % 2 + 1) * 256], op0=ALU.mult, op1=ALU.add))(h, bk),
                    reads=[S.b, bk.b], writes=[S.b])
            if sb_next is not None:
                mk.add("act", lambda e: e.copy(out=sb_next.t[:], in_=S.t[:]), reads=[S.b], writes=[sb_next.b])

        def scores(c, rp):
            r = R[c]
            qrT, krT, qsT = r["qrT"], r["krT"], r["qsT"]
            PT = PT_r[c % 2]
            brs = next_bank()

            def f(e):
                ins = None
                for h in range(4):
                    ins = e.matmul(brs.t[:, h * 128:(h + 1) * 128], lhsT=krT.t[:, h, :], rhs=qrT.t[:, h, :],
                                   start=True, stop=True)
                return ins
            mk.add("pe", f, reads=[krT.b, qrT.b], writes=[brs.b])
            mk.add("dve", lambda e: e.tensor_tensor(out=PT.t[:].rearrange("p h t -> p (h t)"), in0=brs.t[:],
                                                    in1=c_dt.t[:], op=ALU.mult),
                   reads=[brs.b, c_dt.b], writes=[PT.b])
            r["PT"] = PT
            ksT_c, ksT_p = r["ksT"], rp["ksT"]
            pts = {}
            for g in range(2):
                for blk in range(2):
                    kk = ksT_p if blk == 0 else ksT_c
                    bs = next_bank()
                    mk.add("pe", (lambda g, kk, bs: lambda e: e.matmul(
                        bs.t[:], lhsT=kk[g].t[:], rhs=qsT.t[:].rearrange("p h t -> p (h t)"),
                        start=True, stop=True))(g, kk, bs),
                        reads=[kk[g].b, qsT.b], writes=[bs.b])
                    es = es_r[(g * 2 + blk) % 2]
                    pT = pT_r[g * 2 + blk]
                    mk.add("act", (lambda es, bs: lambda e: e.activation(out=es.t[:], in_=bs.t[:], func=AF.Exp,
                                                                         scale=0.125))(es, bs),
                           reads=[bs.b], writes=[es.b])
                    if blk == 0 and c == 0:
                        ec, eb = c_e0.t[:, g * 512:(g + 1) * 512], c_e0.b
                    else:
                        ec, eb = c_e.t[:, (g * 2 + blk) * 512:(g * 2 + blk + 1) * 512], c_e.b
                    mk.add("pool", (lambda pT, es, ec: lambda e: e.tensor_tensor(out=pT.t[:], in0=es.t[:], in1=ec,
                                                                                 op=ALU.mult))(pT, es, ec),
                           reads=[es.b, eb], writes=[pT.b])
                    pts[(g, blk)] = pT
            r["pts"] = pts
            if c >= 1:
                swa_norm(c - 1)

        def tail_a(c, rp):
            r = R[c]
            kv(c, Sbf[(c + 1) % 2])
            qdT, v, PT = r["qdT"], r["v"], r["PT"]
            Sb = Sbf[c % 2]
            bo = [banks[6], banks[7]]
            for h in range(4):
                bk = bo[h // 2]

                def f(e, h=h, bk=bk):
                    e.matmul(bk.t[:, (h % 2) * 256:(h % 2 + 1) * 256], lhsT=PT.t[:, h, :],
                             rhs=v.t[:, h * 256:(h + 1) * 256], start=True, stop=False)
                    return e.matmul(bk.t[:, (h % 2) * 256:(h % 2 + 1) * 256], lhsT=qdT.t[:, h, :],
                                    rhs=Sb.t[:, h * 256:(h + 1) * 256], start=False, stop=True)
                mk.add("pe", f, reads=[PT.b, v.b, qdT.b, Sb.b], writes=[bk.b])
            st, mv, rsd, nb = gn_st[c % 2], gn_mv[c % 2], gn_rs[c % 2], gn_nb[c % 2]
            for h in range(4):
                bk = bo[h // 2]
                mk.add("dve", (lambda h, bk: lambda e: e.bn_stats(out=st.t[:, h, :], in_=bk.t[:, (h % 2) * 256:(h % 2 + 1) * 256]))(h, bk),
                       reads=[bk.b], writes=[st.b])
                mk.add("dve", (lambda h: lambda e: e.bn_aggr(out=mv.t[:, h, :], in_=st.t[:, h, :]))(h),
                       reads=[st.b], writes=[mv.b])
            mk.add("pool", lambda e: e.tensor_scalar(out=nb.t[:], in0=mv.t[:, :, 1], scalar1=EPS, scalar2=None, op0=ALU.add),
                   reads=[mv.b], writes=[nb.b])
            mk.add("pool", lambda e: e.tensor_tensor(out=rsd.t[:], in0=nb.t[:], in1=mhalf.t[:], op=ALU.pow),
                   reads=[nb.b, mhalf.b], writes=[rsd.b])
            r["bo"] = bo
            pts = r["pts"]
            vs_c, vs_p = r["vs"], rp["vs"]
            bot = banks[4]
            bden = banks[5]
            def f(e):
                ins = None
                n_ = 0
                for g in range(2):
                    for blk in range(2):
                        vv = vs_p if blk == 0 else vs_c
                        e.matmul(bot.t[:], lhsT=vv[g].t[:], rhs=pts[(g, blk)].t[:], start=(n_ == 0), stop=(n_ == 3))
                        n_ += 1
                n_ = 0
                for g in range(2):
                    for blk in range(2):
                        ins = e.matmul(bden.t[:], lhsT=onesm[g].t[:], rhs=pts[(g, blk)].t[:], start=(n_ == 0), stop=(n_ == 3))
                        n_ += 1
                return ins
            mk.add("pe", f, reads=[vs_p[0].b, vs_p[1].b, vs_c[0].b, vs_c[1].b] + [pts[k_].b for k_ in pts] +
                   [onesm[0].b, onesm[1].b], writes=[bot.b, bden.b])
            dn = dn_r[c % 2]
            botS = botS_r[c % 2]
            mk.add("act", lambda e: e.copy(out=botS.t[:], in_=bot.t[:]), reads=[bot.b], writes=[botS.b])
            mk.add("dve", lambda e: e.tensor_tensor(out=dn.t[:], in0=bden.t[:], in1=sinkc.t[:], op=ALU.add),
                   reads=[bden.b, sinkc.b], writes=[dn.b])
            if c >= 1:
                ret_tr(c - 1)
            mk.add("dve", lambda e: e.scalar_tensor_tensor(out=nb.t[:], in0=mv.t[:, :, 0], scalar=-1.0, in1=rsd.t[:],
                                                           op0=ALU.mult, op1=ALU.mult),
                   reads=[mv.b, rsd.b], writes=[nb.b])

        def tail_b(c):
            r = R[c]
            sg = r["sg"]
            bo = r["bo"]
            rsd, nb = gn_rs[c % 2], gn_nb[c % 2]
            on = on_r[c % 2]
            ret = ret_r[c % 2]
            for h in range(4):
                bk = bo[h // 2]
                mk.add("act", (lambda h, bk: lambda e: e.activation(
                    out=on.t[:, h * 256:(h + 1) * 256], in_=bk.t[:, (h % 2) * 256:(h % 2 + 1) * 256], func=AF.Identity,
                    bias=nb.t[:, h:h + 1], scale=rsd.t[:, h:h + 1]))(h, bk),
                    reads=[bk.b, nb.b, rsd.b], writes=[on.b])
            mk.add("pool", lambda e: e.tensor_tensor(out=ret.t[:], in0=on.t[:], in1=sg.t[:], op=ALU.mult),
                   reads=[on.b, sg.b], writes=[ret.b])
            r["ret"] = ret

        def swa_norm(c):
            dn, dn2, botS, swaT = dn_r[c % 2], dn2_r[c % 2], botS_r[c % 2], swaT_r[c % 2]
            mk.add("act", lambda e: e.activation(out=dn2.t[:], in_=dn.t[:], func=AF.Ln), reads=[dn.b], writes=[dn2.b])
            mk.add("act", lambda e: e.activation(out=dn2.t[:], in_=dn2.t[:], func=AF.Exp, scale=-1.0),
                   reads=[dn2.b], writes=[dn2.b])
            mk.add("pool", lambda e: e.tensor_tensor(out=swaT.t[:], in0=botS.t[:], in1=dn2.t[:], op=ALU.mult),
                   reads=[botS.b, dn2.b], writes=[swaT.b])
            pending_stores.append((lambda e: e.dma_start(out=swaT_d[c], in_=swaT.t[:]), [swaT.b],
                                   [dbuf("swaT%d" % c)]))

        def ret_tr(c):
            ret = R[c]["ret"]
            retT = retT_r[c % 2]
            transpose8(ret, retT.t[:], retT, "dve")
            pending_stores.append((lambda e: e.dma_start(out=retT_d[c], in_=retT.t[:].rearrange("p k t -> p (k t)")),
                                   [retT.b], [dbuf("retT%d" % c)]))

        NA = NT if "A" in phases else 0
        seq = list(range(-NP, NA))

        def src_of(ci):
            if ci < 0:
                pc = ci + NP
                return xp_d[pc * 128:(pc + 1) * 128, :]
            return x_d[ci * 128:(ci + 1) * 128, :]

        if NP == 0:
            R[-1] = dict(ksT=ksT_r[2], vs=vs_r[2])
        n = len(seq)
        bank_pool[0] = [0, 1, 2, 3]
        if n:
            first = seq[0]
            last = seq[-1]
            for j in range(3):
                if first + j <= last:
                    load(first + j, src_of(first + j))
            norm(first)
            if first + 1 <= last:
                norm(first + 1)
            emit_w(24 if NP > 0 else 1000)
            tr(first)
            proj(first, first >= 0, first >= -1)
            if first + 1 <= last:
                tr(first + 1)
            for ci in seq:
                flush_stores()
                emit_w(1000 if ci >= -2 else 6)
                if ci + 3 <= last:
                    load(ci + 3, src_of(ci + 3))
                if ci + 2 <= last:
                    norm(ci + 2)
                if ci >= 0:
                    scores(ci, R[ci - 1])
                nx = ci + 1
                if nx <= last:
                    proj(nx, nx >= 0, nx >= -1)
                if nx == last and NA:
                    prefetch_B()
                if ci + 2 <= last:
                    tr(ci + 2)
                if ci >= 1:
                    tail_b(ci - 1)
                if ci >= 0:
                    tail_a(ci, R[ci - 1])
                else:
                    kv(ci, Sbf[0] if ci == -1 else None)
                if (ci - 3) in R and ci - 3 != -1:
                    del R[ci - 3]
            if NA:
                swa_norm(NA - 1)
                tail_b(NA - 1)
                ret_tr(NA - 1)
            flush_stores()
        sb.reset(m0)


    bank_pool[0] = list(range(8))
    if "B" in phases:
        mk.barrier()
        small_ring.clear()
        m0 = sb.mark()
        have_pf = bool(pfB_global)
        if have_pf:
            wro, wso, wg = pfB["wro"], pfB["wso"], pfB["wg"]
            sb.off += 16384 + 8192 + 32768
            sb.peak = max(sb.peak, sb.off)
        else:
            wro = sb.alloc([128, 8, 1024], BF16, "wro")
            wso = sb.alloc([128, 4, 1024], BF16, "wso")
            wg = sb.alloc([128, 8, 2048], BF16, "wg")
        wo = sb.alloc([128, 8, 1024], BF16, "wo")
        gpm = sb.alloc([128, D], F32, "gpm")
        xTg_r = [sb.alloc([128, 8, 512], BF16, "xTg") for _ in range(2)]
        retTg_r = [sb.alloc([128, 8, 512], BF16, "retTg") for _ in range(2)]
        swaTg_r = [sb.alloc([128, 4, 512], BF16, "swaTg") for _ in range(2)]
        sgr_r = [sb.alloc([128, 512], F32, "sgr") for _ in range(2)]
        sgs_r = [sb.alloc([128, 512], F32, "sgs") for _ in range(2)]
        t1_r = [sb.alloc([128, 512], F32, "t1") for _ in range(2)]
        t2_r = [sb.alloc([128, 512], F32, "t2") for _ in range(2)]
        mT_r = [sb.alloc([128, 8, 512], BF16, "mT") for _ in range(2)]
        xres_r = [sb.alloc([128, D], F32, "xres") for _ in range(3)]
        xo_r = [sb.alloc([128, D], F32, "xo") for _ in range(2)]
        load_bcast(gpm, g_pm_d)
        if have_pf:
            b_wro, b_wso, b_wg = pfB["b_wro"], pfB["b_wso"], pfB["b_wg"]
        else:
            b_wro = [load_w(wro.t[:, k, :], w_ro_d[k * 128:(k + 1) * 128, :], "wro%d" % k) for k in range(8)]
            b_wso = [load_w(wso.t[g * 64:(g + 1) * 64, :, :],
                            w_so_d[g * 256:(g + 1) * 256, :].rearrange("(r d) c -> d r c", r=4, d=64), "wso%d" % g)
                     for g in range(2)]
            b_wg = []
            for k in range(8):
                for hf in range(2):
                    b_wg.append(load_w(wg.t[:, k, hf * 1024:(hf + 1) * 1024],
                                       w_in_d[k * 128:(k + 1) * 128, 3840 + hf * 1024:3840 + (hf + 1) * 1024], "wg"))
        b_wo = [load_w(wo.t[:, k, :], w_o_d[k * 128:(k + 1) * 128, :], "wo%d" % k) for k in range(8)]
        NG = NT // 4
        mT_b = [[Buf("mT%d_%d" % (i, dt)) for dt in range(8)] for i in range(2)]

        def b_loads(g):
            xTg, retTg, swaTg = xTg_r[g % 2], retTg_r[g % 2], swaTg_r[g % 2]
            for cc in range(4):
                c = g * 4 + cc
                mk.add("sp", (lambda c, cc: lambda e: e.dma_start(
                    out=xTg.t[:, :, cc * 128:(cc + 1) * 128], in_=xT_d[c].rearrange("p (k t) -> p k t", k=8)))(c, cc),
                    reads=[dbuf("xT%d" % c)], writes=[xTg.b], dma=True)
                mk.add("sp", (lambda c, cc: lambda e: e.dma_start(
                    out=retTg.t[:, :, cc * 128:(cc + 1) * 128], in_=retT_d[c].rearrange("p (k t) -> p k t", k=8)))(c, cc),
                    reads=[dbuf("retT%d" % c)], writes=[retTg.b], dma=True)
                mk.add("sp", (lambda c, cc: lambda e: e.dma_start(
                    out=swaTg.t[:, :, cc * 128:(cc + 1) * 128], in_=swaT_d[c].rearrange("p (k t) -> p k t", k=4)))(c, cc),
                    reads=[dbuf("swaT%d" % c)], writes=[swaTg.b], dma=True)

        def b_loads2(g):
            res = {}
            for nm, ring, src_d, kk in (("xT", xTg_r, xT_d, 8), ("retT", retTg_r, retT_d, 8), ("swaT", swaTg_r, swaT_d, 4)):
                tl = ring[g % 2]
                bl = []
                for cc in range(4):
                    c = g * 4 + cc
                    op = mk.add("sp", (lambda c, cc, tl, src_d, kk: lambda e: e.dma_start(
                        out=tl.t[:, :, cc * 128:(cc + 1) * 128], in_=src_d[c].rearrange("p (k t) -> p k t", k=kk)))(c, cc, tl, src_d, kk),
                        reads=[dbuf("%s%d" % (nm, c))], writes=[tl.b], dma=True)
                    b_ = Buf("ld")
                    b_.lw = op
                    bl.append(b_)
                res[nm] = (tl, bl)
            return res

        def b_merge(g, ld, dts):
            (xTg, bx), (retTg, br), (swaTg, bs_) = ld["xT"], ld["retT"], ld["swaT"]
            for dt in dts:
                b_merge1(g, ld, dt)

        def b_merge1(g, ld, dt):
            (xTg, bx), (retTg, br), (swaTg, bs_) = ld["xT"], ld["retT"], ld["swaT"]
            mT = mT_r[g % 2]
            if True:
                bYR, bYS, bGR, bGS = next_bank(), next_bank(), next_bank(), next_bank()
                dsl = slice(dt * 128, (dt + 1) * 128)

                def fyr(e):
                    ins = None
                    for k in range(8):
                        ins = e.matmul(bYR.t[:], lhsT=wro.t[:, k, dsl], rhs=retTg.t[:, k, :], start=(k == 0), stop=(k == 7))
                    return ins
                mk.add("pe", fyr, reads=[retTg.b] + br + b_wro, writes=[bYR.b])

                def fys(e):
                    ins = None
                    for k in range(4):
                        ins = e.matmul(bYS.t[:], lhsT=wso.t[:, k, dsl], rhs=swaTg.t[:, k, :], start=(k == 0), stop=(k == 3))
                    return ins
                mk.add("pe", fys, reads=[swaTg.b] + bs_ + b_wso, writes=[bYS.b])

                def fgr(e):
                    ins = None
                    for k in range(8):
                        ins = e.matmul(bGR.t[:], lhsT=wg.t[:, k, dsl], rhs=xTg.t[:, k, :], start=(k == 0), stop=(k == 7))
                    return ins
                mk.add("pe", fgr, reads=[xTg.b] + bx + b_wg, writes=[bGR.b])
                dsl2 = slice(1024 + dt * 128, 1024 + (dt + 1) * 128)

                def fgs(e):
                    ins = None
                    for k in range(8):
                        ins = e.matmul(bGS.t[:], lhsT=wg.t[:, k, dsl2], rhs=xTg.t[:, k, :], start=(k == 0), stop=(k == 7))
                    return ins
                mk.add("pe", fgs, reads=[xTg.b] + bx + b_wg, writes=[bGS.b])
                sgr, sgs, t1, t2 = sgr_r[dt % 2], sgs_r[dt % 2], t1_r[dt % 2], t2_r[dt % 2]
                mk.add("act", lambda e: e.activation(out=sgr.t[:], in_=bGR.t[:], func=AF.Sigmoid), reads=[bGR.b], writes=[sgr.b])
                mk.add("act", lambda e: e.activation(out=sgs.t[:], in_=bGS.t[:], func=AF.Sigmoid), reads=[bGS.b], writes=[sgs.b])
                mk.add("dve", lambda e: e.tensor_tensor(out=t1.t[:], in0=bYR.t[:], in1=sgr.t[:], op=ALU.mult),
                       reads=[bYR.b, sgr.b], writes=[t1.b])
                mk.add("dve", lambda e: e.tensor_tensor(out=t2.t[:], in0=bYS.t[:], in1=sgs.t[:], op=ALU.mult),
                       reads=[bYS.b, sgs.b], writes=[t2.b])
                mk.add("dve", lambda e: e.tensor_tensor(out=mT.t[:, dt, :], in0=t1.t[:], in1=t2.t[:], op=ALU.add),
                       reads=[t1.b, t2.b], writes=[mT_b[g % 2][dt]])

        def b_out(g):
            for cc in range(4):
                b_out1(g, cc)

        def b_out1(g, cc):
            mT = mT_r[g % 2]
            if True:
                c = g * 4 + cc
                bz = [next_bank(), next_bank()]
                for ct in range(2):
                    for half in range(2):
                        def f(e, ct=ct, half=half):
                            ins = None
                            for dt in range(half * 4, half * 4 + 4):
                                ins = e.matmul(bz[ct].t[:], lhsT=mT.t[:, dt, cc * 128:(cc + 1) * 128],
                                               rhs=wo.t[:, dt, ct * 512:(ct + 1) * 512], start=(dt == 0), stop=(dt == 7))
                            return ins
                        mk.add("pe", f, reads=mT_b[g % 2][half * 4:half * 4 + 4] + b_wo, writes=[bz[ct].b])
                xres = xres_r[c % 3]
                xo = xo_r[c % 2]
                mk.add("sp", lambda e: e.dma_start(out=xres.t[:], in_=x_d[c * 128:(c + 1) * 128, :]), writes=[xres.b], dma=True)
                post_norm_residual(bz[0], bz[1], xres, gpm, xo, "b")
                mk.add("pool", lambda e: e.dma_start(out=out_d[c * 128:(c + 1) * 128, :], in_=xo.t[:]), reads=[xo.b],
                       writes=[dbuf("x1_%d" % c)], dma=True)

        def prefetch_C():
            if "C" not in phases:
                return
            wupA_ = T(nc.alloc_sbuf_tensor_at("wupA", [128, 2, 8, 1024], BF16, offset=195584), "wupA")
            assert sb.off + 1024 <= 195584, sb.off
            pfC_global["wupA"] = wupA_
            pfC_global["b"] = [[load_w(wupA_.t[:, q, k, :], w_up_d[k * 128:(k + 1) * 128, q * 1024:(q + 1) * 1024], "wup")
                                for k in range(8)] for q in range(2)]

        lds = {0: b_loads2(0)}
        for g in range(NG):
            if g == max(0, NG - 2):
                prefetch_C()
            if g + 1 < NG:
                lds[g + 1] = b_loads2(g + 1)
            b_merge(g, lds[g], range(0, 2))
            if g >= 1:
                b_out(g - 1)
            b_merge(g, lds[g], range(2, 8))
        b_out(NG - 1)
        sb.reset(m0)

    if "C" in phases:
        mk.barrier()
        small_ring.clear()
        m0 = sb.mark()
        if pfC_global:
            wupA = pfC_global["wupA"]
        else:
            wupA = T(nc.alloc_sbuf_tensor_at("wupA", [128, 2, 8, 1024], BF16, offset=195584), "wupA")
        wupB = sb.alloc([128, 2, 8, 1024], BF16, "wupB")
        wups = [wupA, wupB]
        wdn = sb.alloc([128, 32, 1024], BF16, "wdn")
        gpl = sb.alloc([128, D], F32, "gpl")
        gpml = sb.alloc([128, D], F32, "gpml")
        CG = 2
        GT = CG * 128
        xin_r = [sb.alloc([128, D], F32, "cxin") for _ in range(4)]
        xs_r = [sb.alloc([128, D], BF16, "cxs") for _ in range(2)]
        xTg_r = [sb.alloc([128, 8, GT], BF16, "cxT") for _ in range(2)]
        rr_r = [sb.alloc([128, GT], F32, "rr") for _ in range(2)]
        uT = sb.alloc([128, 32, GT], BF16, "uT")
        uT_b = [Buf("uT%d" % f) for f in range(32)]
        xres_r = [sb.alloc([128, D], F32, "cxres") for _ in range(2)]
        xo_r = [sb.alloc([128, D], F32, "cxo") for _ in range(2)]
        load_bcast(gpl, g_pl_d)
        load_bcast(gpml, g_pml_d)
        NGc = NT // CG
        xsrc = out_d if "B" in phases else x_d

        def c_load(g):
            for cc in range(CG):
                c = g * CG + cc
                xin = xin_r[c % 4]
                mk.add("sp", (lambda c, xin: lambda e: e.dma_start(out=xin.t[:], in_=xsrc[c * 128:(c + 1) * 128, :]))(c, xin),
                       reads=[dbuf("x1_%d" % c)], writes=[xin.b], dma=True)

        def c_norm(g):
            for cc in range(CG):
                c = g * CG + cc
                rms_front(xin_r[c % 4], xs_r[c % 2], gpl, "c")

        def c_tr(g):
            xTg = xTg_r[g % 2]
            for cc in range(CG):
                c = g * CG + cc
                transpose8(xs_r[c % 2], xTg.t[:, :, cc * 128:(cc + 1) * 128], xTg, "act")
            return xTg

        def c_prep(g):
            c_norm(g)
            return c_tr(g)

        def c_up(g, xTg, fts):
            for ft in fts:
                bk = next_bank()

                def f(e, ft=ft, bk=bk):
                    ins = None
                    for k in range(8):
                        ins = e.matmul(bk.t[:, 0:GT], lhsT=wups[ft // 16].t[:, (ft // 8) % 2, k, (ft % 8) * 128:(ft % 8 + 1) * 128],
                                       rhs=xTg.t[:, k, :], start=(k == 0), stop=(k == 7))
                    return ins
                mk.add("pe", f, reads=[xTg.b] + b_wup[ft // 8], writes=[bk.b])
                rr = rr_r[ft % 2]
                mk.add("act", (lambda rr, bk: lambda e: e.activation(out=rr.t[:], in_=bk.t[:, 0:GT], func=AF.Relu))(rr, bk),
                       reads=[bk.b], writes=[rr.b])
                mk.add("dve", (lambda rr, ft: lambda e: e.tensor_tensor(out=uT.t[:, ft, :], in0=rr.t[:], in1=rr.t[:],
                                                                        op=ALU.mult))(rr, ft),
                       reads=[rr.b], writes=[uT_b[ft]])

        def c_down(g):
            for cc in range(CG):
                c_down1(g, cc)

        def c_down1(g, cc):
            if True:
                c = g * CG + cc
                bz = [next_bank(), next_bank()]
                for ct in range(2):
                    for q in range(4):
                        def f(e, ct=ct, q=q):
                            ins = None
                            for ff in range(q * 8, q * 8 + 8):
                                ins = e.matmul(bz[ct].t[:], lhsT=uT.t[:, ff, cc * 128:(cc + 1) * 128],
                                               rhs=wdn.t[:, ff, ct * 512:(ct + 1) * 512], start=(ff == 0), stop=(ff == 31))
                            return ins
                        mk.add("pe", f, reads=uT_b[q * 8:q * 8 + 8] + b_wdn[q * 2:q * 2 + 2], writes=[bz[ct].b])
                xres = xres_r[c % 2]
                xo = xo_r[c % 2]
                mk.add("sp", (lambda c, xres: lambda e: e.dma_start(out=xres.t[:], in_=xsrc[c * 128:(c + 1) * 128, :]))(c, xres),
                       reads=[dbuf("x1_%d" % c)], writes=[xres.b], dma=True)
                post_norm_residual(bz[0], bz[1], xres, gpml, xo, "d")
                mk.add("pool", (lambda c, xo: lambda e: e.dma_start(out=out_d[c * 128:(c + 1) * 128, :], in_=xo.t[:]))(c, xo),
                       reads=[xo.b], writes=[dbuf("x1_%d" % c)], dma=True)

        c_load(0)
        if NGc > 1:
            c_load(1)
        xt = {0: c_prep(0)}
        b_wup = []
        for q in range(4):
            if q < 2 and pfC_global:
                b_wup.append(pfC_global["b"][q])
            else:
                b_wup.append([load_w(wups[q // 2].t[:, q % 2, k, :],
                                     w_up_d[k * 128:(k + 1) * 128, q * 1024:(q + 1) * 1024], "wup") for k in range(8)])
        b_wdn = [load_w(wdn.t[:, j * 4:(j + 1) * 4, :],
                        w_dn_d[j * 512:(j + 1) * 512, :].rearrange("(j p) c -> p j c", p=128), "wdn") for j in range(8)]
        for g in range(NGc):
            if g + 2 < NGc:
                c_load(g + 2)
            if g + 1 < NGc and g > 0:
                c_norm(g + 1)
            c_up(g, xt[g], range(0, 16))
            if g + 1 < NGc:
                if g == 0:
                    c_norm(g + 1)
                xt[g + 1] = c_tr(g + 1)
            c_up(g, xt[g], range(16, 32))
            dump("c_xT", xt[g].t[:].rearrange("p k t -> p (k t)"), [128, 8 * GT], BF16, [xt[g].b])
            dump("c_uT", uT.t[:].rearrange("p k t -> p (k t)"), [128, 32 * GT], BF16, uT_b)
            c_down(g)
        sb.reset(m0)

    mk.emit()
    return nc


_W_KEYS = ("w_in", "w_ret_out", "w_swa_out", "w_out", "w_up", "w_down", "pre_mix_norm", "post_mix_norm",
           "pre_mlp_norm", "post_mlp_norm", "sinks")


def make_in_maps(inputs, ncores, NT, NP):
    x = np.asarray(inputs["x"], dtype=np.float32)
    B, Tseq, _ = x.shape
    per_seq = Tseq // (NT * 128)
    assert B * per_seq == ncores
    w = {}
    for k in _W_KEYS:
        a = np.asarray(inputs[k], dtype=np.float32)
        w[k] = np.ascontiguousarray(a[0] if a.ndim == 3 else a.reshape(1, -1))
    cfirst, _ = _consts(True)
    cother, _ = _consts(False)
    maps = []
    for c in range(ncores):
        b, h = divmod(c, per_seq)
        t0 = h * NT * 128
        m = dict(w)
        m["x"] = np.ascontiguousarray(x[b, t0:t0 + NT * 128])
        if h == 0:
            m["xprev"] = np.zeros((NP * 128, D), np.float32)
        else:
            assert t0 >= NP * 128
            m["xprev"] = np.ascontiguousarray(x[b, t0 - NP * 128:t0])
        m.update(cfirst if h == 0 else cother)
        maps.append(m)
    return maps


_CACHE = {}


def kernel(x, pre_mix_norm, w_in, w_ret_out, w_swa_out, w_out, sinks, post_mix_norm, pre_mlp_norm, w_up, w_down,
           post_mlp_norm):
    inputs = dict(x=x, pre_mix_norm=pre_mix_norm, w_in=w_in, w_ret_out=w_ret_out, w_swa_out=w_swa_out, w_out=w_out,
                  sinks=sinks, post_mix_norm=post_mix_norm, pre_mlp_norm=pre_mlp_norm, w_up=w_up, w_down=w_down,
                  post_mlp_norm=post_mlp_norm)
    NT, NP, ncores = 32, 32, 8
    nc = build(NT, NP)
    maps = make_in_maps(inputs, ncores, NT, NP)
    res = run_bass_kernel_spmd(nc, maps, core_ids=list(range(ncores)))
    B, Tseq, _ = np.asarray(x).shape
    out = np.empty((B, Tseq, D), np.float32)
    per_seq = Tseq // (NT * 128)
    for c in range(ncores):
        b, h = divmod(c, per_seq)
        out[b, h * NT * 128:(h + 1) * NT * 128] = res.results[c]["out"]
    return out
```

```python
import numpy as np
import ml_dtypes
import concourse.bass as bass
import concourse.mybir as mybir
from concourse.bass_utils import run_bass_kernel_spmd

F32 = mybir.dt.float32
BF16 = mybir.dt.bfloat16
ALU = mybir.AluOpType
AF = mybir.ActivationFunctionType

D = 1024
EPS = 1e-6
ENGS = ("pe", "act", "dve", "pool", "sp")


class Buf:
    __slots__ = ("name", "lw", "rd", "rd_dma", "excl")

    def __init__(self, name, excl=False):
        self.name = name
        self.excl = excl
        self.lw = None
        self.rd = {}
        self.rd_dma = []


class Op:
    __slots__ = ("eng", "fn", "dma", "idx", "deps_c", "deps_d", "needs_inc", "seq", "dsem", "dval", "dprev")

    def __init__(self, eng, fn, dma):
        self.eng = eng
        self.fn = fn
        self.dma = dma
        self.needs_inc = False
        self.seq = None
        self.dsem = None
        self.dval = None
        self.dprev = None


class MK:
    DMA_SLOTS = {"sp": 24, "pool": 16}

    def __init__(self, nc):
        self.nc = nc
        self.ops = {e: [] for e in ENGS}
        self.ndma = {e: 0 for e in ENGS}
        self.last_dma = {}
        self.bar = None
        self.bar_pending = set()

    def barrier(self):
        last_c = {e: self.ops[e][-1] for e in ("pe", "act", "dve", "pool")
                  if any(not o.dma for o in self.ops[e])}
        for e in list(last_c):
            o = [o for o in self.ops[e] if not o.dma][-1]
            last_c[e] = o
        self.bar = (last_c, list(self.last_dma.values()))
        self.bar_pending = set(ENGS)

    def add(self, eng, fn, reads=(), writes=(), dma=False):
        op = Op(eng, fn, dma)
        op.idx = len(self.ops[eng])
        deps_c = {}
        deps_d = []

        def dep(o):
            if o is None:
                return
            if o.dma:
                if o not in deps_d:
                    deps_d.append(o)
            else:
                cur = deps_c.get(o.eng)
                if cur is None or o.idx > cur.idx:
                    deps_c[o.eng] = o

        for b in reads:
            dep(b.lw)
            if b.excl:
                for e2, o in b.rd.items():
                    if e2 != eng:
                        dep(o)
        for b in writes:
            dep(b.lw)
            for o in b.rd.values():
                dep(o)
            for o in b.rd_dma:
                dep(o)
        if eng in self.bar_pending:
            self.bar_pending.discard(eng)
            for o in self.bar[0].values():
                dep(o)
            for o in self.bar[1]:
                dep(o)
        if eng == "pe":
            deps_c.pop("pe", None)
        if dma:
            n = self.DMA_SLOTS[eng]
            self.last_dma[(eng, self.ndma[eng] % n)] = op
            self.ndma[eng] += 1
        for o in deps_c.values():
            o.needs_inc = True
        op.deps_c = deps_c
        op.deps_d = deps_d
        for b in reads:
            if dma:
                b.rd_dma.append(op)
            else:
                b.rd[eng] = op
        for b in writes:
            b.lw = op
            b.rd = {}
            b.rd_dma = []
        self.ops[eng].append(op)
        return op

    def emit(self):
        n_dma_sems = tuple(self.DMA_SLOTS.items())
        nc = self.nc
        prog = {e: nc.alloc_semaphore("prog_" + e) for e in ("pe", "act", "dve", "pool")}
        pools = {q: [nc.alloc_semaphore("dma_%s_%d" % (q, i)) for i in range(n)] for q, n in n_dma_sems}
        for e in ENGS:
            cnt = 0
            ndma = 0
            last = {}
            for op in self.ops[e]:
                if op.dma:
                    pool = pools[e]
                    i = ndma % len(pool)
                    ndma += 1
                    op.dsem = pool[i]
                    prev = last.get(i, 0)
                    op.dprev = prev
                    op.dval = prev + 16
                    last[i] = op.dval
                elif op.needs_inc:
                    cnt += 1
                    op.seq = cnt
        handles = {"pe": nc.tensor, "act": nc.scalar, "dve": nc.vector, "pool": nc.gpsimd, "sp": nc.sync}
        final_dma = []
        for e in ENGS:
            for op in self.ops[e]:
                if op.dma:
                    final_dma.append(op)

        def run_engine(e, h):
            waited = {}

            def wait(sem, val):
                key = id(sem)
                if waited.get(key, 0) >= val:
                    return
                h.wait_ge(sem, val)
                waited[key] = val

            for op in self.ops[e]:
                for d in op.deps_c.values():
                    wait(prog[d.eng], d.seq)
                for d in op.deps_d:
                    wait(d.dsem, d.dval)
                if op.dma:
                    if op.dprev:
                        wait(op.dsem, op.dprev)
                    ins = op.fn(h)
                    ins.then_inc(op.dsem, 16)
                else:
                    ins = op.fn(h)
                    if op.needs_inc:
                        ins.then_inc(prog[e], 1)
            if e == "sp":
                lastval = {}
                for op in final_dma:
                    k = id(op.dsem)
                    if k not in lastval or lastval[k][1] < op.dval:
                        lastval[k] = (op.dsem, op.dval)
                for sem, val in lastval.values():
                    wait(sem, val)

        with nc.Block() as block:
            @block.tensor
            def _(h):
                run_engine("pe", h)

            @block.scalar
            def _(h):
                run_engine("act", h)

            @block.vector
            def _(h):
                run_engine("dve", h)

            @block.gpsimd
            def _(h):
                run_engine("pool", h)

            @block.sync
            def _(h):
                run_engine("sp", h)


class T:
    __slots__ = ("t", "b")

    def __init__(self, t, name, excl=False):
        self.t = t
        self.b = Buf(name, excl)


def _dsize(dt):
    return 4 if dt == F32 else 2


class SBAlloc:
    def __init__(self, nc):
        self.nc = nc
        self.off = 16384 + 256
        self.n = 0
        self.peak = 0

    def alloc(self, shape, dtype, name="t"):
        nb = _dsize(dtype)
        for s in shape[1:]:
            nb *= s
        nb = (nb + 31) // 32 * 32
        t = self.nc.alloc_sbuf_tensor_at("%s_%d" % (name, self.n), list(shape), dtype, offset=self.off)
        self.n += 1
        self.off += nb
        self.peak = max(self.peak, self.off)
        assert self.off <= 229376, ("SBUF overflow", self.off)
        return T(t, name)

    def mark(self):
        return self.off

    def reset(self, m):
        self.off = m


def _consts(first_half):
    H = 4
    C = 128
    lg = np.log(1.0 - np.exp2(-5.0 - np.arange(H, dtype=np.float64)))
    i = np.arange(C, dtype=np.float64)
    dk = 128.0 ** -0.5
    diff = i[None, :] - i[:, None]
    DT = np.zeros((C, H, C))
    for h in range(H):
        DT[:, h, :] = np.where(diff >= 0, np.exp(lg[h] * np.where(diff >= 0, diff, 0.0)), 0.0) * dk
    GQ = np.zeros((C, H, C))
    for h in range(H):
        GQ[:, h, :] = np.exp(lg[h] * (i + 1.0))[None, :]
    KD = np.zeros((C, H, C))
    for h in range(H):
        KD[:, h, :] = (np.exp(lg[h] * (C - 1.0 - i)) * dk)[:, None]
    cd = [float(np.exp(lg[h] * C)) for h in range(H)]
    Hq = 8
    slopes = np.exp2(-8.0 / Hq * np.arange(1, Hq + 1, dtype=np.float64))
    E = np.zeros((2, 2, C, 4, C))
    for g in range(2):
        for r in range(4):
            sl = slopes[g * 4 + r]
            dist = i[None, :] - i[:, None]
            E[g, 1, :, r, :] = np.where(dist >= 0, np.exp(-sl * np.where(dist >= 0, dist, 0.0)), 0.0)
            dist = i[None, :] + 128.0 - i[:, None]
            E[g, 0, :, r, :] = np.where(dist < 128, np.exp(-sl * np.where(dist < 128, dist, 0.0)), 0.0)
    E0 = E[:, 0].copy()
    if first_half:
        E0[:] = 0.0
    ident = np.eye(128, dtype=np.float32).astype(ml_dtypes.bfloat16)
    return dict(
        c_dt=DT.reshape(C, 512).astype(np.float32),
        c_gq=GQ.reshape(C, 512).astype(np.float32),
        c_kd=KD.reshape(C, 512).astype(np.float32),
        c_e=np.ascontiguousarray(E.transpose(2, 0, 1, 3, 4)).reshape(C, 4 * 512).astype(np.float32),
        c_e0=np.ascontiguousarray(E0.transpose(1, 0, 2, 3)).reshape(C, 2 * 512).astype(np.float32),
        c_ident=ident,
    ), cd


def build(NT=32, NP=32, phases="PABC", dbg=False):
    nc = bass.Bass("TRN2", target_bir_lowering=False)
    mk = MK(nc)
    sb = SBAlloc(nc)
    _, cd = _consts(True)
    Tn = NT * 128

    def din(name, shape, dt=F32):
        return nc.dram_tensor(name, list(shape), dt, kind="ExternalInput").ap()

    x_d = din("x", [Tn, D])
    xp_d = din("xprev", [NP * 128, D])
    w_in_d = din("w_in", [D, 5888])
    w_ro_d = din("w_ret_out", [1024, D])
    w_so_d = din("w_swa_out", [512, D])
    w_o_d = din("w_out", [D, D])
    w_up_d = din("w_up", [D, 4096])
    w_dn_d = din("w_down", [4096, D])
    g_pre_d = din("pre_mix_norm", [1, D])
    g_pm_d = din("post_mix_norm", [1, D])
    g_pl_d = din("pre_mlp_norm", [1, D])
    g_pml_d = din("post_mlp_norm", [1, D])
    sinks_d = din("sinks", [1, 8])
    c_dt_d = din("c_dt", [128, 512])
    c_gq_d = din("c_gq", [128, 512])
    c_kd_d = din("c_kd", [128, 512])
    c_e_d = din("c_e", [128, 2048])
    c_e0_d = din("c_e0", [128, 1024])
    c_id_d = din("c_ident", [128, 128], BF16)
    out_d = nc.dram_tensor("out", [Tn, D], F32, kind="ExternalOutput").ap()
    skind = "ExternalOutput" if dbg else "Internal"
    xT_d = nc.dram_tensor("xT_s", [NT, 128, 8 * 128], BF16, kind=skind).ap()
    retT_d = nc.dram_tensor("retT_s", [NT, 128, 8 * 128], BF16, kind=skind).ap()
    swaT_d = nc.dram_tensor("swaT_s", [NT, 128, 4 * 128], BF16, kind=skind).ap()
    dram_b = {}

    def dbuf(name):
        if name not in dram_b:
            dram_b[name] = Buf(name)
        return dram_b[name]

    dumped = set()

    def dump(name, ap, shape, dt, reads):
        if not dbg or name in dumped:
            return
        dumped.add(name)
        d_ = nc.dram_tensor("dbg_" + name, list(shape), dt, kind="ExternalOutput").ap()
        mk.add("sp", lambda e: e.dma_start(out=d_, in_=ap), reads=reads, writes=[dbuf("dbg_" + name)], dma=True)

    pfB_global = {}
    pfC_global = {}
    banks = [T(nc.alloc_psum_tensor("bank%d" % i, [128, 512], F32), "bank%d" % i, True) for i in range(8)]
    bank_ctr = [0]
    bank_pool = [list(range(8))]

    def next_bank():
        p = bank_pool[0]
        b = banks[p[bank_ctr[0] % len(p)]]
        bank_ctr[0] += 1
        return b

    def bf(bank):
        return bank.t[:].bitcast(BF16)

    ident = sb.alloc([128, 128], BF16, "ident")
    mk.add("sp", lambda e: e.dma_start(out=ident.t[:], in_=c_id_d[:, :]), writes=[ident.b], dma=True)
    base_mark = sb.mark()

    def load_bcast(dst, src_d):
        mk.add("sp", lambda e: e.dma_start(out=dst.t[:], in_=src_d[0:1, :].partition_broadcast(128)),
               writes=[dst.b], dma=True)

    def load_w(dst_ap, src_ap, name):
        b_ = Buf(name)
        mk.add("pool", lambda e: e.dma_start(out=dst_ap, in_=src_ap), writes=[b_], dma=True)
        return b_

    def rms_front(xin, xs, gam, tag):
        ss = small(tag + "ss")
        rs = small(tag + "rs")
        mk.add("act", lambda e: e.activation(out=xs.t[:], in_=xin.t[:], func=AF.Square, accum_out=ss.t[:, 0:1]),
               reads=[xin.b], writes=[xs.b, ss.b])
        mk.add("pool", lambda e: e.tensor_scalar(out=rs.t[:, 0:1], in0=ss.t[:, 0:1], scalar1=1.0 / D, scalar2=EPS,
                                                 op0=ALU.mult, op1=ALU.add),
               reads=[ss.b], writes=[rs.b])
        mk.add("pool", lambda e: e.tensor_tensor(out=rs.t[:, 1:2], in0=rs.t[:, 0:1], in1=mhalf.t[:, 0:1], op=ALU.pow),
               reads=[rs.b, mhalf.b], writes=[rs.b])
        mk.add("dve", lambda e: e.scalar_tensor_tensor(out=xs.t[:], in0=xin.t[:], scalar=rs.t[:, 1:2],
                                                       in1=gam.t[:], op0=ALU.mult, op1=ALU.mult),
               reads=[xin.b, rs.b, gam.b], writes=[xs.b])

    def transpose8(src, dst_ap, dst, eng="act"):
        bk = next_bank()
        bv = bf(bk)

        def f(e):
            ins = None
            for k in range(8):
                ins = e.transpose(out=bv[:, k * 128:(k + 1) * 128], in_=src.t[:, k * 128:(k + 1) * 128],
                                  identity=ident.t[:])
            return ins

        mk.add("pe", f, reads=[src.b, ident.b], writes=[bk.b])
        src_ap = bv[:, 0:1024].rearrange("p (k t) -> p k t", k=8)
        if eng == "act":
            mk.add("act", lambda e: e.copy(out=dst_ap, in_=src_ap), reads=[bk.b], writes=[dst.b])
        else:
            mk.add("dve", lambda e: e.tensor_copy(out=dst_ap, in_=src_ap), reads=[bk.b], writes=[dst.b])

    def post_norm_residual(bk0, bk1, xres, gam, xo, tag):
        ss = small(tag + "ss")
        rs = small(tag + "rs")
        junk = junk_t
        mk.add("act", lambda e: e.activation(out=junk.t[:, 0:512], in_=bk0.t[:], func=AF.Square,
                                             accum_out=ss.t[:, 0:1]),
               reads=[bk0.b], writes=[junk.b, ss.b])
        mk.add("act", lambda e: e.activation(out=junk.t[:, 512:1024], in_=bk1.t[:], func=AF.Square,
                                             accum_out=ss.t[:, 1:2]),
               reads=[bk1.b], writes=[junk.b, ss.b])
        mk.add("dve", lambda e: e.tensor_tensor(out=ss.t[:, 2:3], in0=ss.t[:, 0:1], in1=ss.t[:, 1:2], op=ALU.add),
               reads=[ss.b], writes=[ss.b])
        mk.add("pool", lambda e: e.tensor_scalar(out=rs.t[:, 0:1], in0=ss.t[:, 2:3], scalar1=1.0 / D, scalar2=EPS,
                                                 op0=ALU.mult, op1=ALU.add),
               reads=[ss.b], writes=[rs.b])
        mk.add("pool", lambda e: e.tensor_tensor(out=rs.t[:, 1:2], in0=rs.t[:, 0:1], in1=mhalf.t[:, 0:1], op=ALU.pow),
               reads=[rs.b, mhalf.b], writes=[rs.b])
        mk.add("dve", lambda e: e.scalar_tensor_tensor(out=xo.t[:, 0:512], in0=bk0.t[:], scalar=rs.t[:, 1:2],
                                                       in1=gam.t[:, 0:512], op0=ALU.mult, op1=ALU.mult),
               reads=[bk0.b, rs.b, gam.b], writes=[xo.b])
        mk.add("dve", lambda e: e.scalar_tensor_tensor(out=xo.t[:, 512:1024], in0=bk1.t[:], scalar=rs.t[:, 1:2],
                                                       in1=gam.t[:, 512:1024], op0=ALU.mult, op1=ALU.mult),
               reads=[bk1.b, rs.b, gam.b], writes=[xo.b])
        dump(tag + "_zn", xo.t[:], [128, 1024], F32, [xo.b])
        dump(tag + "_gam", gam.t[:], [128, 1024], F32, [gam.b])
        dump(tag + "_rs", rs.t[:], [128, 4], F32, [rs.b])
        dump(tag + "_ss", ss.t[:], [128, 4], F32, [ss.b])
        if False and dbg and (tag + "_zr") not in dumped:
            zr = sb.alloc([128, 1024], F32, "zr")
            mk.add("act", lambda e: e.copy(out=zr.t[:, 0:512], in_=bk0.t[:]), reads=[bk0.b], writes=[zr.b])
            mk.add("act", lambda e: e.copy(out=zr.t[:, 512:1024], in_=bk1.t[:]), reads=[bk1.b], writes=[zr.b])
            dump(tag + "_zr", zr.t[:], [128, 1024], F32, [zr.b])
        mk.add("dve", lambda e: e.tensor_tensor(out=xo.t[:], in0=xo.t[:], in1=xres.t[:], op=ALU.add),
               reads=[xo.b, xres.b], writes=[xo.b])

    small_ring = {}

    def small(tag):
        if tag not in small_ring:
            small_ring[tag] = ([sb_small.alloc([128, 4], F32, tag) for _ in range(4)], [0])
        lst, i = small_ring[tag]
        t = lst[i[0] % 4]
        i[0] += 1
        return t

    sb_small = sb
    epst = sb.alloc([128, 1], F32, "eps")
    mk.add("pool", lambda e: e.memset(epst.t[:], EPS), writes=[epst.b])
    junk_t = sb.alloc([128, 1024], BF16, "junk")
    mhalf = sb.alloc([128, 4], F32, "mhalf")
    mk.add("pool", lambda e: e.memset(mhalf.t[:], -0.5), writes=[mhalf.b])
    if "A" in phases or "P" in phases:
        m0 = sb.mark()
        winA_off = sb.off
        winA = sb.alloc([128, 8, 3840], BF16, "winA")
        sb_fence = sb.alloc([128, 8], F32, "fence")
        gpre = sb.alloc([128, D], F32, "gpre")
        c_dt = sb.alloc([128, 512], F32, "c_dt")
        c_gq = sb.alloc([128, 512], F32, "c_gq")
        c_kd = sb.alloc([128, 512], F32, "c_kd")
        c_e = sb.alloc([128, 2048], F32, "c_e")
        c_e0 = sb.alloc([128, 1024], F32, "c_e0")
        ones64 = sb.alloc([128, 64], BF16, "ones64")
        sinkc = sb.alloc([128, 512], F32, "sinkc")
        sk8 = sb.alloc([128, 8], F32, "sk8")
        S = sb.alloc([128, 1024], F32, "S")
        Sbf = [sb.alloc([128, 1024], BF16, "Sbf%d" % i) for i in range(2)]
        xin_r = [sb.alloc([128, D], F32, "xin") for _ in range(4)]
        xs_r = [sb.alloc([128, D], BF16, "xs") for _ in range(2)]
        xT_r = [sb.alloc([128, 8, 128], BF16, "xT") for _ in range(3)]
        qrT_r = [sb.alloc([128, 4, 128], BF16, "qrT") for _ in range(2)]
        qdT_r = [sb.alloc([128, 4, 128], BF16, "qdT") for _ in range(2)]
        krT_r = [sb.alloc([128, 4, 128], BF16, "krT") for _ in range(2)]
        qsT_r = [sb.alloc([128, 4, 128], BF16, "qsT") for _ in range(2)]
        ksT_r = [[sb.alloc([128, 128], BF16, "ksT") for g in range(2)] for _ in range(3)]
        vs_r = [[sb.alloc([128, 128], BF16, "vs") for g in range(2)] for _ in range(3)]
        onesm = [sb.alloc([128, 128], BF16, "onesm") for g in range(2)]
        kdec_r = [sb.alloc([128, 512], BF16, "kdec") for _ in range(2)]
        v_r = [sb.alloc([128, 1024], BF16, "v") for _ in range(2)]
        sg_r = [sb.alloc([128, 1024], F32, "sg") for _ in range(3)]
        PT_r = [sb.alloc([128, 4, 128], BF16, "PT") for _ in range(2)]
        on_r = [sb.alloc([128, 1024], F32, "on") for _ in range(2)]
        ret_r = [sb.alloc([128, 1024], BF16, "ret") for _ in range(2)]
        retT_r = [sb.alloc([128, 8, 128], BF16, "retT") for _ in range(2)]
        es_r = [sb.alloc([128, 512], F32, "es") for _ in range(2)]
        pT_r = [sb.alloc([128, 512], BF16, "pT") for _ in range(4)]
        dn_r = [sb.alloc([128, 512], F32, "dn") for _ in range(2)]
        dn2_r = [sb.alloc([128, 512], F32, "dn2") for _ in range(2)]
        botS_r = [sb.alloc([128, 512], F32, "botS") for _ in range(2)]
        swaT_r = [sb.alloc([128, 512], BF16, "swaT") for _ in range(2)]
        gn_st = [sb.alloc([128, 4, 6], F32, "gnst") for _ in range(2)]
        gn_mv = [sb.alloc([128, 4, 2], F32, "gnmv") for _ in range(2)]
        gn_rs = [sb.alloc([128, 4], F32, "gnrs") for _ in range(2)]
        gn_nb = [sb.alloc([128, 4], F32, "gnnb") for _ in range(2)]

        for t_, d_ in ((c_dt, c_dt_d), (c_gq, c_gq_d), (c_kd, c_kd_d), (c_e, c_e_d), (c_e0, c_e0_d)):
            mk.add("sp", (lambda t_, d_: lambda e: e.dma_start(out=t_.t[:], in_=d_[:, :]))(t_, d_),
                   writes=[t_.b], dma=True)
        load_bcast(gpre, g_pre_d)
        mk.add("sp", lambda e: e.dma_start(out=sk8.t[:], in_=sinks_d[0:1, :].partition_broadcast(128)),
               writes=[sk8.b], dma=True)
        col_order = [("kr", 512, 1024), ("vr", 1024, 2048), ("ksvs", 3584, 3840), ("qr", 0, 512),
                     ("gr", 2048, 3072), ("qs", 3072, 3584)]
        wA = {}
        wq = []
        for (nm, c0, c1) in col_order:
            wA[nm] = []
            for k in range(8):
                b_ = Buf("winA_%s_%d" % (nm, k))
                wA[nm].append(b_)
                if nm == "qs":
                    for t4 in range(4):
                        if t4 > 0:
                            b_ = Buf("winA_qs_%d_%d" % (k, t4))
                            wA[nm].append(b_)
                        wq.append((nm, (lambda k, t4: lambda e: e.dma_start(
                            out=winA.t[:, k, 3072 + t4 * 128:3072 + (t4 + 1) * 128].rearrange("p (g d) -> p g d", g=2, d=64),
                            in_=w_in_d[k * 128:(k + 1) * 128, 3072:3584].rearrange(
                                "p (g t d) -> p t g d", g=2, t=4, d=64)[:, t4]))(k, t4), b_))
                    continue
                wq.append((nm, (lambda k, c0, c1: lambda e: e.dma_start(
                    out=winA.t[:, k, c0:c1], in_=w_in_d[k * 128:(k + 1) * 128, c0:c1]))(k, c0, c1), b_))

        def emit_w(n_):
            for _ in range(n_):
                if wq:
                    nm_, fn_, b__ = wq.pop(0)
                    mk.add("pool", fn_, writes=[b__], dma=True)
        mk.add("pool", lambda e: e.memset(ones64.t[:], 1.0), writes=[ones64.b])
        for g in range(2):
            mk.add("pool", (lambda g: lambda e: e.memset(onesm[g].t[:], 0.0))(g), writes=[onesm[g].b])
            mk.add("pool", (lambda g: lambda e: e.memset(onesm[g].t[:, g * 64:(g + 1) * 64], 1.0))(g), writes=[onesm[g].b])
            for i3 in range(3):
                mk.add("pool", (lambda t_: lambda e: e.memset(t_.t[:], 0.0))(ksT_r[i3][g]), writes=[ksT_r[i3][g].b])
                mk.add("pool", (lambda t_: lambda e: e.memset(t_.t[:], 0.0))(vs_r[i3][g]), writes=[vs_r[i3][g].b])
        mk.add("pool", lambda e: e.memset(S.t[:], 0.0), writes=[S.b])
        mk.add("pool", lambda e: e.memset(Sbf[0].t[:], 0.0), writes=[Sbf[0].b])
        mk.add("act", lambda e: e.activation(out=sk8.t[:], in_=sk8.t[:], func=AF.Exp), reads=[sk8.b], writes=[sk8.b])
        for g in range(2):
            for r in range(4):
                mk.add("dve", (lambda g, r: lambda e: e.tensor_copy(
                    out=sinkc.t[g * 64:(g + 1) * 64, r * 128:(r + 1) * 128],
                    in_=sk8.t[g * 64:(g + 1) * 64, g * 4 + r:g * 4 + r + 1].to_broadcast([64, 128])))(g, r),
                    reads=[sk8.b], writes=[sinkc.b])

        QR, KR, VR, GR, QS, KS, VS = 0, 512, 1024, 2048, 3072, 3584, 3712

        R = {}
        winA_guard = Buf("winA_guard")
        pfB = pfB_global

        def prefetch_B():
            if "B" not in phases:
                return
            fence = sb_fence
            mk.add("pool", lambda e: e.memset(fence.t[:], 0.0), writes=[fence.b, winA_guard])
            base = winA_off
            wro_ = T(nc.alloc_sbuf_tensor_at("wro_pf", [128, 8, 1024], BF16, offset=base), "wro")
            wso_ = T(nc.alloc_sbuf_tensor_at("wso_pf", [128, 4, 1024], BF16, offset=base + 16384), "wso")
            wg_ = T(nc.alloc_sbuf_tensor_at("wg_pf", [128, 8, 2048], BF16, offset=base + 24576), "wg")
            pfB["wro"], pfB["wso"], pfB["wg"] = wro_, wso_, wg_
            pfB["b_wro"] = [load_w(wro_.t[:, k, :], w_ro_d[k * 128:(k + 1) * 128, :], "wro%d" % k) for k in range(8)]
            pfB["b_wso"] = [load_w(wso_.t[g * 64:(g + 1) * 64, :, :],
                                   w_so_d[g * 256:(g + 1) * 256, :].rearrange("(r d) c -> d r c", r=4, d=64), "wso%d" % g)
                            for g in range(2)]
            bl = []
            for k in range(8):
                for hf in range(2):
                    bl.append(load_w(wg_.t[:, k, hf * 1024:(hf + 1) * 1024],
                                     w_in_d[k * 128:(k + 1) * 128, 3840 + hf * 1024:3840 + (hf + 1) * 1024], "wg"))
            pfB["b_wg"] = bl

        pending_stores = []

        def flush_stores():
            for fn, rd, wr in pending_stores:
                mk.add("sp", fn, reads=rd, writes=wr, dma=True)
            del pending_stores[:]

        def load(ci, src_ap):
            xin = xin_r[ci % 4]
            mk.add("sp", lambda e: e.dma_start(out=xin.t[:], in_=src_ap), writes=[xin.b], dma=True)

        def norm(ci):
            xin = xin_r[ci % 4]
            xs = xs_r[ci % 2]
            rms_front(xin, xs, gpre, "f")
            R[ci] = dict(xs=xs)

        def tr(ci):
            xT = xT_r[ci % 3]
            transpose8(R[ci]["xs"], xT.t[:], xT, "act")
            R[ci]["xT"] = xT
            if ci >= 0:
                pending_stores.append((lambda e: e.dma_start(out=xT_d[ci], in_=xT.t[:].rearrange("p k t -> p (k t)")),
                                       [xT.b], [dbuf("xT%d" % ci)]))

        def proj(ci, full, want_swa_kv):
            r = R[ci]
            xT = r["xT"]

            def proj_tok(c0, ncols, bk, boff=0, wb=()):
                def f(e):
                    ins = None
                    for k in range(8):
                        ins = e.matmul(bk.t[:, boff:boff + ncols], lhsT=xT.t[:, k, :], rhs=winA.t[:, k, c0:c0 + ncols],
                                       start=(k == 0), stop=(k == 7))
                    return ins
                mk.add("pe", f, reads=[xT.b, winA_guard] + list(wb), writes=[bk.b])

            def proj_feat(c0, bk, boff, wb=()):
                def f(e):
                    ins = None
                    for k in range(8):
                        ins = e.matmul(bk.t[:, boff:boff + 128], lhsT=winA.t[:, k, c0:c0 + 128], rhs=xT.t[:, k, :],
                                       start=(k == 0), stop=(k == 7))
                    return ins
                mk.add("pe", f, reads=[xT.b, winA_guard] + list(wb), writes=[bk.b])

            if full:
                qrT = qrT_r[ci % 2]
                qdT = qdT_r[ci % 2]
                krT = krT_r[ci % 2]
                qsT = qsT_r[ci % 2]
                bq = next_bank()
                for h in range(4):
                    proj_feat(QR + h * 128, bq, h * 128, wA["qr"])
                mk.add("act", lambda e: e.copy(out=qrT.t[:].rearrange("p h t -> p (h t)"), in_=bq.t[:]),
                       reads=[bq.b], writes=[qrT.b])
                mk.add("dve", lambda e: e.tensor_tensor(out=qdT.t[:].rearrange("p h t -> p (h t)"), in0=bq.t[:],
                                                        in1=c_gq.t[:], op=ALU.mult),
                       reads=[bq.b, c_gq.b], writes=[qdT.b])
                bkk = next_bank()
                for h in range(4):
                    proj_feat(KR + h * 128, bkk, h * 128, wA["kr"])
                mk.add("dve", lambda e: e.tensor_copy(out=krT.t[:].rearrange("p h t -> p (h t)"), in_=bkk.t[:]),
                       reads=[bkk.b], writes=[krT.b])
                bqs = next_bank()
                for t4 in range(4):
                    proj_feat(QS + t4 * 128, bqs, t4 * 128, wA["qs"])
                mk.add("act", lambda e: e.copy(out=qsT.t[:].rearrange("p h t -> p (h t)"), in_=bqs.t[:]),
                       reads=[bqs.b], writes=[qsT.b])
                r.update(qrT=qrT, qdT=qdT, krT=krT, qsT=qsT)
            if want_swa_kv:
                ksT = ksT_r[ci % 3]
                vs = vs_r[ci % 3]
                bks = next_bank()
                proj_feat(KS, bks, 0, wA["ksvs"])
                proj_tok(VS, 128, bks, 128, wA["ksvs"])
                for g in range(2):
                    mk.add("dve", (lambda g: lambda e: e.tensor_copy(out=ksT[g].t[g * 64:(g + 1) * 64, :],
                                                                     in_=bks.t[g * 64:(g + 1) * 64, 0:128]))(g),
                           reads=[bks.b], writes=[ksT[g].b])
                    mk.add("dve", (lambda g: lambda e: e.tensor_copy(out=vs[g].t[:, g * 64:(g + 1) * 64],
                                                                     in_=bks.t[:, 128 + g * 64:128 + (g + 1) * 64]))(g),
                           reads=[bks.b], writes=[vs[g].b])
                r.update(ksT=ksT, vs=vs)
            kdec = kdec_r[ci % 2]
            v = v_r[ci % 2]
            bk = next_bank()
            proj_tok(KR, 512, bk, 0, wA["kr"])
            mk.add("dve", lambda e: e.tensor_tensor(out=kdec.t[:], in0=bk.t[:], in1=c_kd.t[:], op=ALU.mult),
                   reads=[bk.b, c_kd.b], writes=[kdec.b])
            for hf in range(2):
                bkv = next_bank()
                proj_tok(VR + hf * 512, 512, bkv, 0, wA["vr"])
                if hf == 0:
                    mk.add("act", (lambda bkv, hf: lambda e: e.copy(out=v.t[:, hf * 512:(hf + 1) * 512], in_=bkv.t[:]))(bkv, hf),
                           reads=[bkv.b], writes=[v.b])
                else:
                    mk.add("dve", (lambda bkv, hf: lambda e: e.tensor_copy(out=v.t[:, hf * 512:(hf + 1) * 512], in_=bkv.t[:]))(bkv, hf),
                           reads=[bkv.b], writes=[v.b])
            r.update(kdec=kdec, v=v)
            if full:
                sg = sg_r[ci % 3]
                for hf in range(2):
                    bg = next_bank()
                    proj_tok(GR + hf * 512, 512, bg, 0, wA["gr"])
                    mk.add("act", (lambda bg, hf: lambda e: e.activation(out=sg.t[:, hf * 512:(hf + 1) * 512], in_=bg.t[:],
                                                                         func=AF.Silu))(bg, hf),
                           reads=[bg.b], writes=[sg.b])
                r.update(sg=sg)

        def kv(ci, sb_next):
            r = R[ci]
            kdec, v = r["kdec"], r["v"]
            bks2 = [banks[4], banks[5]]
            for h in range(4):
                bk = bks2[h // 2]
                mk.add("pe", (lambda h, bk: lambda e: e.matmul(
                    bk.t[:, (h % 2) * 256:(h % 2 + 1) * 256], lhsT=kdec.t[:, h * 128:(h + 1) * 128],
                    rhs=v.t[:, h * 256:(h + 1) * 256], start=True, stop=True))(h, bk),
                    reads=[kdec.b, v.b], writes=[bk.b])
            for h in range(4):
                bk = bks2[h // 2]
                mk.add("dve", (lambda h, bk: lambda e: e.scalar_tensor_tensor(
                    out=S.t[:, h * 256:(h + 1) * 256], in0=S.t[:, h * 256:(h + 1) * 256], scalar=cd[h],
                    in1=bk.t[:, (h % 2) * 256:(h % 2 + 1) * 256], op0=ALU.mult, op1=ALU.add))(h, bk),
                    reads=[S.b, bk.b], writes=[S.b])
            if sb_next is not None:
                mk.add("act", lambda e: e.copy(out=sb_next.t[:], in_=S.t[:]), reads=[S.b], writes=[sb_next.b])

        def scores(c, rp):
            r = R[c]
            qrT, krT, qsT = r["qrT"], r["krT"], r["qsT"]
            PT = PT_r[c % 2]
            brs = next_bank()

            def f(e):
                ins = None
                for h in range(4):
                    ins = e.matmul(brs.t[:, h * 128:(h + 1) * 128], lhsT=krT.t[:, h, :], rhs=qrT.t[:, h, :],
                                   start=True, stop=True)
                return ins
            mk.add("pe", f, reads=[krT.b, qrT.b], writes=[brs.b])
            mk.add("dve", lambda e: e.tensor_tensor(out=PT.t[:].rearrange("p h t -> p (h t)"), in0=brs.t[:],
                                                    in1=c_dt.t[:], op=ALU.mult),
                   reads=[brs.b, c_dt.b], writes=[PT.b])
            r["PT"] = PT
            ksT_c, ksT_p = r["ksT"], rp["ksT"]
            pts = {}
            for g in range(2):
                for blk in range(2):
                    kk = ksT_p if blk == 0 else ksT_c
                    bs = next_bank()
                    mk.add("pe", (lambda g, kk, bs: lambda e: e.matmul(
                        bs.t[:], lhsT=kk[g].t[:], rhs=qsT.t[:].rearrange("p h t -> p (h t)"),
                        start=True, stop=True))(g, kk, bs),
                        reads=[kk[g].b, qsT.b], writes=[bs.b])
                    es = es_r[(g * 2 + blk) % 2]
                    pT = pT_r[g * 2 + blk]
                    mk.add("act", (lambda es, bs: lambda e: e.activation(out=es.t[:], in_=bs.t[:], func=AF.Exp,
                                                                         scale=0.125))(es, bs),
                           reads=[bs.b], writes=[es.b])
                    if blk == 0 and c == 0:
                        ec, eb = c_e0.t[:, g * 512:(g + 1) * 512], c_e0.b
                    else:
                        ec, eb = c_e.t[:, (g * 2 + blk) * 512:(g * 2 + blk + 1) * 512], c_e.b
                    mk.add("pool", (lambda pT, es, ec: lambda e: e.tensor_tensor(out=pT.t[:], in0=es.t[:], in1=ec,
                                                                                 op=ALU.mult))(pT, es, ec),
                           reads=[es.b, eb], writes=[pT.b])
                    pts[(g, blk)] = pT
            r["pts"] = pts
            if c >= 1:
                swa_norm(c - 1)

        def tail_a(c, rp):
            r = R[c]
            kv(c, Sbf[(c + 1) % 2])
            qdT, v, PT = r["qdT"], r["v"], r["PT"]
            Sb = Sbf[c % 2]
            bo = [banks[6], banks[7]]
            for h in range(4):
                bk = bo[h // 2]

                def f(e, h=h, bk=bk):
                    e.matmul(bk.t[:, (h % 2) * 256:(h % 2 + 1) * 256], lhsT=PT.t[:, h, :],
                             rhs=v.t[:, h * 256:(h + 1) * 256], start=True, stop=False)
                    return e.matmul(bk.t[:, (h % 2) * 256:(h % 2 + 1) * 256], lhsT=qdT.t[:, h, :],
                                    rhs=Sb.t[:, h * 256:(h + 1) * 256], start=False, stop=True)
                mk.add("pe", f, reads=[PT.b, v.b, qdT.b, Sb.b], writes=[bk.b])
            st, mv, rsd, nb = gn_st[c % 2], gn_mv[c % 2], gn_rs[c % 2], gn_nb[c % 2]
            for h in range(4):
                bk = bo[h // 2]
                mk.add("dve", (lambda h, bk: lambda e: e.bn_stats(out=st.t[:, h, :], in_=bk.t[:, (h % 2) * 256:(h % 2 + 1) * 256]))(h, bk),
                       reads=[bk.b], writes=[st.b])
                mk.add("dve", (lambda h: lambda e: e.bn_aggr(out=mv.t[:, h, :], in_=st.t[:, h, :]))(h),
                       reads=[st.b], writes=[mv.b])
            mk.add("pool", lambda e: e.tensor_scalar(out=nb.t[:], in0=mv.t[:, :, 1], scalar1=EPS, scalar2=None, op0=ALU.add),
                   reads=[mv.b], writes=[nb.b])
            mk.add("pool", lambda e: e.tensor_tensor(out=rsd.t[:], in0=nb.t[:], in1=mhalf.t[:], op=ALU.pow),
                   reads=[nb.b, mhalf.b], writes=[rsd.b])
            r["bo"] = bo
            pts = r["pts"]
            vs_c, vs_p = r["vs"], rp["vs"]
            bot = banks[4]
            bden = banks[5]
            def f(e):
                ins = None
                n_ = 0
                for g in range(2):
                    for blk in range(2):
                        vv = vs_p if blk == 0 else vs_c
                        e.matmul(bot.t[:], lhsT=vv[g].t[:], rhs=pts[(g, blk)].t[:], start=(n_ == 0), stop=(n_ == 3))
                        n_ += 1
                n_ = 0
                for g in range(2):
                    for blk in range(2):
                        ins = e.matmul(bden.t[:], lhsT=onesm[g].t[:], rhs=pts[(g, blk)].t[:], start=(n_ == 0), stop=(n_ == 3))
                        n_ += 1
                return ins
            mk.add("pe", f, reads=[vs_p[0].b, vs_p[1].b, vs_c[0].b, vs_c[1].b] + [pts[k_].b for k_ in pts] +
                   [onesm[0].b, onesm[1].b], writes=[bot.b, bden.b])
            dn = dn_r[c % 2]
            botS = botS_r[c % 2]
            mk.add("act", lambda e: e.copy(out=botS.t[:], in_=bot.t[:]), reads=[bot.b], writes=[botS.b])
            mk.add("dve", lambda e: e.tensor_tensor(out=dn.t[:], in0=bden.t[:], in1=sinkc.t[:], op=ALU.add),
                   reads=[bden.b, sinkc.b], writes=[dn.b])
            if c >= 1:
                ret_tr(c - 1)
            mk.add("dve", lambda e: e.scalar_tensor_tensor(out=nb.t[:], in0=mv.t[:, :, 0], scalar=-1.0, in1=rsd.t[:],
                                                           op0=ALU.mult, op1=ALU.mult),
                   reads=[mv.b, rsd.b], writes=[nb.b])

        def tail_b(c):
            r = R[c]
            sg = r["sg"]
            bo = r["bo"]
            rsd, nb = gn_rs[c % 2], gn_nb[c % 2]
            on = on_r[c % 2]
            ret = ret_r[c % 2]
            for h in range(4):
                bk = bo[h // 2]
                mk.add("act", (lambda h, bk: lambda e: e.activation(
                    out=on.t[:, h * 256:(h + 1) * 256], in_=bk.t[:, (h % 2) * 256:(h % 2 + 1) * 256], func=AF.Identity,
                    bias=nb.t[:, h:h + 1], scale=rsd.t[:, h:h + 1]))(h, bk),
                    reads=[bk.b, nb.b, rsd.b], writes=[on.b])
            mk.add("pool", lambda e: e.tensor_tensor(out=ret.t[:], in0=on.t[:], in1=sg.t[:], op=ALU.mult),
                   reads=[on.b, sg.b], writes=[ret.b])
            r["ret"] = ret

        def swa_norm(c):
            dn, dn2, botS, swaT = dn_r[c % 2], dn2_r[c % 2], botS_r[c % 2], swaT_r[c % 2]
            mk.add("act", lambda e: e.activation(out=dn2.t[:], in_=dn.t[:], func=AF.Ln), reads=[dn.b], writes=[dn2.b])
            mk.add("act", lambda e: e.activation(out=dn2.t[:], in_=dn2.t[:], func=AF.Exp, scale=-1.0),
                   reads=[dn2.b], writes=[dn2.b])
            mk.add("pool", lambda e: e.tensor_tensor(out=swaT.t[:], in0=botS.t[:], in1=dn2.t[:], op=ALU.mult),
                   reads=[botS.b, dn2.b], writes=[swaT.b])
            pending_stores.append((lambda e: e.dma_start(out=swaT_d[c], in_=swaT.t[:]), [swaT.b],
                                   [dbuf("swaT%d" % c)]))

        def ret_tr(c):
            ret = R[c]["ret"]
            retT = retT_r[c % 2]
            transpose8(ret, retT.t[:], retT, "dve")
            pending_stores.append((lambda e: e.dma_start(out=retT_d[c], in_=retT.t[:].rearrange("p k t -> p (k t)")),
                                   [retT.b], [dbuf("retT%d" % c)]))

        NA = NT if "A" in phases else 0
        seq = list(range(-NP, NA))

        def src_of(ci):
            if ci < 0:
                pc = ci + NP
                return xp_d[pc * 128:(pc + 1) * 128, :]
            return x_d[ci * 128:(ci + 1) * 128, :]

        if NP == 0:
            R[-1] = dict(ksT=ksT_r[2], vs=vs_r[2])
        n = len(seq)
        bank_pool[0] = [0, 1, 2, 3]
        if n:
            first = seq[0]
            last = seq[-1]
            for j in range(3):
                if first + j <= last:
                    load(first + j, src_of(first + j))
            norm(first)
            if first + 1 <= last:
                norm(first + 1)
            emit_w(24 if NP > 0 else 1000)
            tr(first)
            proj(first, first >= 0, first >= -1)
            if first + 1 <= last:
                tr(first + 1)
            for ci in seq:
                flush_stores()
                emit_w(1000 if ci >= -2 else 6)
                if ci + 3 <= last:
                    load(ci + 3, src_of(ci + 3))
                if ci + 2 <= last:
                    norm(ci + 2)
                if ci >= 0:
                    scores(ci, R[ci - 1])
                nx = ci + 1
                if nx <= last:
                    proj(nx, nx >= 0, nx >= -1)
                if nx == last and NA:
                    prefetch_B()
                if ci + 2 <= last:
                    tr(ci + 2)
                if ci >= 1:
                    tail_b(ci - 1)
                if ci >= 0:
                    tail_a(ci, R[ci - 1])
                else:
                    kv(ci, Sbf[0] if ci == -1 else None)
                if (ci - 3) in R and ci - 3 != -1:
                    del R[ci - 3]
            if NA:
                swa_norm(NA - 1)
                tail_b(NA - 1)
                ret_tr(NA - 1)
            flush_stores()
        sb.reset(m0)


    bank_pool[0] = list(range(8))
    if "B" in phases:
        mk.barrier()
        small_ring.clear()
        m0 = sb.mark()
        have_pf = bool(pfB_global)
        if have_pf:
            wro, wso, wg = pfB["wro"], pfB["wso"], pfB["wg"]
            sb.off += 16384 + 8192 + 32768
            sb.peak = max(sb.peak, sb.off)
        else:
            wro = sb.alloc([128, 8, 1024], BF16, "wro")
            wso = sb.alloc([128, 4, 1024], BF16, "wso")
            wg = sb.alloc([128, 8, 2048], BF16, "wg")
        wo = sb.alloc([128, 8, 1024], BF16, "wo")
        gpm = sb.alloc([128, D], F32, "gpm")
        xTg_r = [sb.alloc([128, 8, 512], BF16, "xTg") for _ in range(2)]
        retTg_r = [sb.alloc([128, 8, 512], BF16, "retTg") for _ in range(2)]
        swaTg_r = [sb.alloc([128, 4, 512], BF16, "swaTg") for _ in range(2)]
        sgr_r = [sb.alloc([128, 512], F32, "sgr") for _ in range(2)]
        sgs_r = [sb.alloc([128, 512], F32, "sgs") for _ in range(2)]
        t1_r = [sb.alloc([128, 512], F32, "t1") for _ in range(2)]
        t2_r = [sb.alloc([128, 512], F32, "t2") for _ in range(2)]
        mT_r = [sb.alloc([128, 8, 512], BF16, "mT") for _ in range(2)]
        xres_r = [sb.alloc([128, D], F32, "xres") for _ in range(3)]
        xo_r = [sb.alloc([128, D], F32, "xo") for _ in range(2)]
        load_bcast(gpm, g_pm_d)
        if have_pf:
            b_wro, b_wso, b_wg = pfB["b_wro"], pfB["b_wso"], pfB["b_wg"]
        else:
            b_wro = [load_w(wro.t[:, k, :], w_ro_d[k * 128:(k + 1) * 128, :], "wro%d" % k) for k in range(8)]
            b_wso = [load_w(wso.t[g * 64:(g + 1) * 64, :, :],
                            w_so_d[g * 256:(g + 1) * 256, :].rearrange("(r d) c -> d r c", r=4, d=64), "wso%d" % g)
                     for g in range(2)]
            b_wg = []
            for k in range(8):
                for hf in range(2):
                    b_wg.append(load_w(wg.t[:, k, hf * 1024:(hf + 1) * 1024],
                                       w_in_d[k * 128:(k + 1) * 128, 3840 + hf * 1024:3840 + (hf + 1) * 1024], "wg"))
        b_wo = [load_w(wo.t[:, k, :], w_o_d[k * 128:(k + 1) * 128, :], "wo%d" % k) for k in range(8)]
        NG = NT // 4
        mT_b = [[Buf("mT%d_%d" % (i, dt)) for dt in range(8)] for i in range(2)]

        def b_loads(g):
            xTg, retTg, swaTg = xTg_r[g % 2], retTg_r[g % 2], swaTg_r[g % 2]
            for cc in range(4):
                c = g * 4 + cc
                mk.add("sp", (lambda c, cc: lambda e: e.dma_start(
                    out=xTg.t[:, :, cc * 128:(cc + 1) * 128], in_=xT_d[c].rearrange("p (k t) -> p k t", k=8)))(c, cc),
                    reads=[dbuf("xT%d" % c)], writes=[xTg.b], dma=True)
                mk.add("sp", (lambda c, cc: lambda e: e.dma_start(
                    out=retTg.t[:, :, cc * 128:(cc + 1) * 128], in_=retT_d[c].rearrange("p (k t) -> p k t", k=8)))(c, cc),
                    reads=[dbuf("retT%d" % c)], writes=[retTg.b], dma=True)
                mk.add("sp", (lambda c, cc: lambda e: e.dma_start(
                    out=swaTg.t[:, :, cc * 128:(cc + 1) * 128], in_=swaT_d[c].rearrange("p (k t) -> p k t", k=4)))(c, cc),
                    reads=[dbuf("swaT%d" % c)], writes=[swaTg.b], dma=True)

        def b_loads2(g):
            res = {}
            for nm, ring, src_d, kk in (("xT", xTg_r, xT_d, 8), ("retT", retTg_r, retT_d, 8), ("swaT", swaTg_r, swaT_d, 4)):
                tl = ring[g % 2]
                bl = []
                for cc in range(4):
                    c = g * 4 + cc
                    op = mk.add("sp", (lambda c, cc, tl, src_d, kk: lambda e: e.dma_start(
                        out=tl.t[:, :, cc * 128:(cc + 1) * 128], in_=src_d[c].rearrange("p (k t) -> p k t", k=kk)))(c, cc, tl, src_d, kk),
                        reads=[dbuf("%s%d" % (nm, c))], writes=[tl.b], dma=True)
                    b_ = Buf("ld")
                    b_.lw = op
                    bl.append(b_)
                res[nm] = (tl, bl)
            return res

        def b_merge(g, ld, dts):
            (xTg, bx), (retTg, br), (swaTg, bs_) = ld["xT"], ld["retT"], ld["swaT"]
            for dt in dts:
                b_merge1(g, ld, dt)

        def b_merge1(g, ld, dt):
            (xTg, bx), (retTg, br), (swaTg, bs_) = ld["xT"], ld["retT"], ld["swaT"]
            mT = mT_r[g % 2]
            if True:
                bYR, bYS, bGR, bGS = next_bank(), next_bank(), next_bank(), next_bank()
                dsl = slice(dt * 128, (dt + 1) * 128)

                def fyr(e):
                    ins = None
                    for k in range(8):
                        ins = e.matmul(bYR.t[:], lhsT=wro.t[:, k, dsl], rhs=retTg.t[:, k, :], start=(k == 0), stop=(k == 7))
                    return ins
                mk.add("pe", fyr, reads=[retTg.b] + br + b_wro, writes=[bYR.b])

                def fys(e):
                    ins = None
                    for k in range(4):
                        ins = e.matmul(bYS.t[:], lhsT=wso.t[:, k, dsl], rhs=swaTg.t[:, k, :], start=(k == 0), stop=(k == 3))
                    return ins
                mk.add("pe", fys, reads=[swaTg.b] + bs_ + b_wso, writes=[bYS.b])

                def fgr(e):
                    ins = None
                    for k in range(8):
                        ins = e.matmul(bGR.t[:], lhsT=wg.t[:, k, dsl], rhs=xTg.t[:, k, :], start=(k == 0), stop=(k == 7))
                    return ins
                mk.add("pe", fgr, reads=[xTg.b] + bx + b_wg, writes=[bGR.b])
                dsl2 = slice(1024 + dt * 128, 1024 + (dt + 1) * 128)

                def fgs(e):
                    ins = None
                    for k in range(8):
                        ins = e.matmul(bGS.t[:], lhsT=wg.t[:, k, dsl2], rhs=xTg.t[:, k, :], start=(k == 0), stop=(k == 7))
                    return ins
                mk.add("pe", fgs, reads=[xTg.b] + bx + b_wg, writes=[bGS.b])
                sgr, sgs, t1, t2 = sgr_r[dt % 2], sgs_r[dt % 2], t1_r[dt % 2], t2_r[dt % 2]
                mk.add("act", lambda e: e.activation(out=sgr.t[:], in_=bGR.t[:], func=AF.Sigmoid), reads=[bGR.b], writes=[sgr.b])
                mk.add("act", lambda e: e.activation(out=sgs.t[:], in_=bGS.t[:], func=AF.Sigmoid), reads=[bGS.b], writes=[sgs.b])
                mk.add("dve", lambda e: e.tensor_tensor(out=t1.t[:], in0=bYR.t[:], in1=sgr.t[:], op=ALU.mult),
                       reads=[bYR.b, sgr.b], writes=[t1.b])
                mk.add("dve", lambda e: e.tensor_tensor(out=t2.t[:], in0=bYS.t[:], in1=sgs.t[:], op=ALU.mult),
                       reads=[bYS.b, sgs.b], writes=[t2.b])
                mk.add("dve", lambda e: e.tensor_tensor(out=mT.t[:, dt, :], in0=t1.t[:], in1=t2.t[:], op=ALU.add),
                       reads=[t1.b, t2.b], writes=[mT_b[g % 2][dt]])

        pend_b = []

        def flush_b_stores():
            for fn, rd, wr in pend_b:
                mk.add("pool", fn, reads=rd, writes=wr, dma=True)
            del pend_b[:]

        def b_out(g):
            for cc in range(4):
                b_out1(g, cc)

        def b_out1(g, cc):
            mT = mT_r[g % 2]
            if True:
                c = g * 4 + cc
                bz = [next_bank(), next_bank()]
                for ct in range(2):
                    for half in range(2):
                        def f(e, ct=ct, half=half):
                            ins = None
                            for dt in range(half * 4, half * 4 + 4):
                                ins = e.matmul(bz[ct].t[:], lhsT=mT.t[:, dt, cc * 128:(cc + 1) * 128],
                                               rhs=wo.t[:, dt, ct * 512:(ct + 1) * 512], start=(dt == 0), stop=(dt == 7))
                            return ins
                        mk.add("pe", f, reads=mT_b[g % 2][half * 4:half * 4 + 4] + b_wo, writes=[bz[ct].b])
                xres = xres_r[c % 3]
                xo = xo_r[c % 2]
                mk.add("sp", lambda e: e.dma_start(out=xres.t[:], in_=x_d[c * 128:(c + 1) * 128, :]), writes=[xres.b], dma=True)
                post_norm_residual(bz[0], bz[1], xres, gpm, xo, "b")
                flush_b_stores()
                pend_b.append((lambda e: e.dma_start(out=out_d[c * 128:(c + 1) * 128, :], in_=xo.t[:]), [xo.b],
                               [dbuf("x1_%d" % c)]))

        def prefetch_C():
            if "C" not in phases:
                return
            wupA_ = T(nc.alloc_sbuf_tensor_at("wupA", [128, 2, 8, 1024], BF16, offset=195584), "wupA")
            assert sb.off + 1024 <= 195584, sb.off
            pfC_global["wupA"] = wupA_
            pfC_global["b"] = [[load_w(wupA_.t[:, q, k, :], w_up_d[k * 128:(k + 1) * 128, q * 1024:(q + 1) * 1024], "wup")
                                for k in range(8)] for q in range(2)]

        lds = {0: b_loads2(0)}
        for g in range(NG):
            if g == max(0, NG - 2):
                prefetch_C()
            if g + 1 < NG:
                lds[g + 1] = b_loads2(g + 1)
            b_merge(g, lds[g], range(0, 2))
            if g >= 1:
                b_out(g - 1)
            b_merge(g, lds[g], range(2, 8))
        b_out(NG - 1)
        flush_b_stores()
        sb.reset(m0)

    if "C" in phases:
        mk.barrier()
        small_ring.clear()
        m0 = sb.mark()
        if pfC_global:
            wupA = pfC_global["wupA"]
        else:
            wupA = T(nc.alloc_sbuf_tensor_at("wupA", [128, 2, 8, 1024], BF16, offset=195584), "wupA")
        wupB = sb.alloc([128, 2, 8, 1024], BF16, "wupB")
        wups = [wupA, wupB]
        wdn = sb.alloc([128, 32, 1024], BF16, "wdn")
        gpl = sb.alloc([128, D], F32, "gpl")
        gpml = sb.alloc([128, D], F32, "gpml")
        CG = 2
        GT = CG * 128
        xin_r = [sb.alloc([128, D], F32, "cxin") for _ in range(4)]
        xs_r = [sb.alloc([128, D], BF16, "cxs") for _ in range(2)]
        xTg_r = [sb.alloc([128, 8, GT], BF16, "cxT") for _ in range(2)]
        rr_r = [sb.alloc([128, GT], F32, "rr") for _ in range(2)]
        uT = sb.alloc([128, 32, GT], BF16, "uT")
        uT_b = [Buf("uT%d" % f) for f in range(32)]
        xres_r = [sb.alloc([128, D], F32, "cxres") for _ in range(2)]
        xo_r = [sb.alloc([128, D], F32, "cxo") for _ in range(2)]
        load_bcast(gpl, g_pl_d)
        load_bcast(gpml, g_pml_d)
        NGc = NT // CG
        xsrc = out_d if "B" in phases else x_d

        def c_load(g):
            for cc in range(CG):
                c = g * CG + cc
                xin = xin_r[c % 4]
                mk.add("sp", (lambda c, xin: lambda e: e.dma_start(out=xin.t[:], in_=xsrc[c * 128:(c + 1) * 128, :]))(c, xin),
                       reads=[dbuf("x1_%d" % c)], writes=[xin.b], dma=True)

        def c_norm(g):
            for cc in range(CG):
                c = g * CG + cc
                rms_front(xin_r[c % 4], xs_r[c % 2], gpl, "c")

        def c_tr(g):
            xTg = xTg_r[g % 2]
            for cc in range(CG):
                c = g * CG + cc
                transpose8(xs_r[c % 2], xTg.t[:, :, cc * 128:(cc + 1) * 128], xTg, "act")
            return xTg

        def c_prep(g):
            c_norm(g)
            return c_tr(g)

        def c_up(g, xTg, fts):
            for ft in fts:
                bk = next_bank()

                def f(e, ft=ft, bk=bk):
                    ins = None
                    for k in range(8):
                        ins = e.matmul(bk.t[:, 0:GT], lhsT=wups[ft // 16].t[:, (ft // 8) % 2, k, (ft % 8) * 128:(ft % 8 + 1) * 128],
                                       rhs=xTg.t[:, k, :], start=(k == 0), stop=(k == 7))
                    return ins
                mk.add("pe", f, reads=[xTg.b] + b_wup[ft // 8], writes=[bk.b])
                rr = rr_r[ft % 2]
                mk.add("act", (lambda rr, bk: lambda e: e.activation(out=rr.t[:], in_=bk.t[:, 0:GT], func=AF.Relu))(rr, bk),
                       reads=[bk.b], writes=[rr.b])
                mk.add("dve", (lambda rr, ft: lambda e: e.tensor_tensor(out=uT.t[:, ft, :], in0=rr.t[:], in1=rr.t[:],
                                                                        op=ALU.mult))(rr, ft),
                       reads=[rr.b], writes=[uT_b[ft]])

        def c_down(g):
            for cc in range(CG):
                c_down1(g, cc)

        def c_down1(g, cc):
            if True:
                c = g * CG + cc
                bz = [next_bank(), next_bank()]
                for ct in range(2):
                    for q in range(4):
                        def f(e, ct=ct, q=q):
                            ins = None
                            for ff in range(q * 8, q * 8 + 8):
                                ins = e.matmul(bz[ct].t[:], lhsT=uT.t[:, ff, cc * 128:(cc + 1) * 128],
                                               rhs=wdn.t[:, ff, ct * 512:(ct + 1) * 512], start=(ff == 0), stop=(ff == 31))
                            return ins
                        mk.add("pe", f, reads=uT_b[q * 8:q * 8 + 8] + b_wdn[q * 2:q * 2 + 2], writes=[bz[ct].b])
                xres = xres_r[c % 2]
                xo = xo_r[c % 2]
                mk.add("sp", (lambda c, xres: lambda e: e.dma_start(out=xres.t[:], in_=xsrc[c * 128:(c + 1) * 128, :]))(c, xres),
                       reads=[dbuf("x1_%d" % c)], writes=[xres.b], dma=True)
                post_norm_residual(bz[0], bz[1], xres, gpml, xo, "d")
                mk.add("pool", (lambda c, xo: lambda e: e.dma_start(out=out_d[c * 128:(c + 1) * 128, :], in_=xo.t[:]))(c, xo),
                       reads=[xo.b], writes=[dbuf("x1_%d" % c)], dma=True)

        c_load(0)
        if NGc > 1:
            c_load(1)
        xt = {0: c_prep(0)}
        b_wup = []
        for q in range(4):
            if q < 2 and pfC_global:
                b_wup.append(pfC_global["b"][q])
            else:
                b_wup.append([load_w(wups[q // 2].t[:, q % 2, k, :],
                                     w_up_d[k * 128:(k + 1) * 128, q * 1024:(q + 1) * 1024], "wup") for k in range(8)])
        b_wdn = [load_w(wdn.t[:, j * 4:(j + 1) * 4, :],
                        w_dn_d[j * 512:(j + 1) * 512, :].rearrange("(j p) c -> p j c", p=128), "wdn") for j in range(8)]
        for g in range(NGc):
            if g + 2 < NGc:
                c_load(g + 2)
            if g + 1 < NGc and g > 0:
                c_norm(g + 1)
            c_up(g, xt[g], range(0, 16))
            if g + 1 < NGc:
                if g == 0:
                    c_norm(g + 1)
                xt[g + 1] = c_tr(g + 1)
            c_up(g, xt[g], range(16, 32))
            dump("c_xT", xt[g].t[:].rearrange("p k t -> p (k t)"), [128, 8 * GT], BF16, [xt[g].b])
            dump("c_uT", uT.t[:].rearrange("p k t -> p (k t)"), [128, 32 * GT], BF16, uT_b)
            c_down(g)
        sb.reset(m0)

    mk.emit()
    return nc


_W_KEYS = ("w_in", "w_ret_out", "w_swa_out", "w_out", "w_up", "w_down", "pre_mix_norm", "post_mix_norm",
           "pre_mlp_norm", "post_mlp_norm", "sinks")


def make_in_maps(inputs, ncores, NT, NP):
    x = np.asarray(inputs["x"], dtype=np.float32)
    B, Tseq, _ = x.shape
    per_seq = Tseq // (NT * 128)
    assert B * per_seq == ncores
    w = {}
    for k in _W_KEYS:
        a = np.asarray(inputs[k], dtype=np.float32)
        w[k] = np.ascontiguousarray(a[0] if a.ndim == 3 else a.reshape(1, -1))
    cfirst, _ = _consts(True)
    cother, _ = _consts(False)
    maps = []
    for c in range(ncores):
        b, h = divmod(c, per_seq)
        t0 = h * NT * 128
        m = dict(w)
        m["x"] = np.ascontiguousarray(x[b, t0:t0 + NT * 128])
        if h == 0:
            m["xprev"] = np.zeros((NP * 128, D), np.float32)
        else:
            assert t0 >= NP * 128
            m["xprev"] = np.ascontiguousarray(x[b, t0 - NP * 128:t0])
        m.update(cfirst if h == 0 else cother)
        maps.append(m)
    return maps


_CACHE = {}


def kernel(x, pre_mix_norm, w_in, w_ret_out, w_swa_out, w_out, sinks, post_mix_norm, pre_mlp_norm, w_up, w_down,
           post_mlp_norm):
    inputs = dict(x=x, pre_mix_norm=pre_mix_norm, w_in=w_in, w_ret_out=w_ret_out, w_swa_out=w_swa_out, w_out=w_out,
                  sinks=sinks, post_mix_norm=post_mix_norm, pre_mlp_norm=pre_mlp_norm, w_up=w_up, w_down=w_down,
                  post_mlp_norm=post_mlp_norm)
    NT, NP, ncores = 32, 32, 8
    nc = build(NT, NP)
    maps = make_in_maps(inputs, ncores, NT, NP)
    res = run_bass_kernel_spmd(nc, maps, core_ids=list(range(ncores)))
    B, Tseq, _ = np.asarray(x).shape
    out = np.empty((B, Tseq, D), np.float32)
    per_seq = Tseq // (NT * 128)
    for c in range(ncores):
        b, h = divmod(c, per_seq)
        out[b, h * NT * 128:(h + 1) * NT * 128] = res.results[c]["out"]
    return out
```
